# Optimizing a Trainium2 kernel written in Bass

```python
import jax, jax.numpy as jnp
from jax import lax
import numpy as np

D_MODEL = 1024
BATCH = 8
SEQ = 2048
DEPTH = 1
DEC_BATCH = 128
DEC_SEQ = 1
PAST_LEN = 16384
PAGE_SIZE = 128

POOL_WIDTH = D_MODEL // 2
POOL_WINDOWS = (2, 4, 8, 16)
POOL_GROUPS = len(POOL_WINDOWS)
POOL_GROUP_DIM = POOL_WIDTH // POOL_GROUPS
POOL_OUT_GROUP_DIM = D_MODEL // POOL_GROUPS
POOL_BUF = max(POOL_WINDOWS) - 1
SSM_WIDTH = D_MODEL // 2
SSM_GROUP_DIM = 16
SSM_GROUPS = SSM_WIDTH // SSM_GROUP_DIM
SSM_STATE = 64
DT_MIN = 1e-3
DT_MAX = 1e-1
D_FF = 4 * D_MODEL
PLE_DIM = 256
EPS = 1e-6
IN_WIDTH = POOL_WIDTH + SSM_WIDTH + 2 * D_MODEL

kernel_name = 'pool_s5_gated_hybrid_step'


def rmsnorm(x, g):
    xf = x.astype(jnp.float32)
    y = xf * lax.rsqrt(jnp.mean(xf * xf, axis=-1, keepdims=True) + EPS)
    return (y * g.astype(jnp.float32)).astype(x.dtype)


def pool_mixer(u, buf, pos_start, w_pool, pool_scale):
    n, l, _ = u.shape
    lb = buf.shape[1]
    ext = jnp.concatenate([buf, u], axis=1).astype(jnp.float32)
    l_ext = ext.shape[1]
    pos = pos_start - lb + jnp.arange(l_ext, dtype=jnp.int32)
    groups = ext.reshape(n, l_ext, POOL_GROUPS, POOL_GROUP_DIM)
    outs = []
    for gi, w in enumerate(POOL_WINDOWS):
        xg = groups[:, :, gi]
        s = jnp.cumsum(xg, axis=1)
        s_lag = jnp.pad(s, ((0, 0), (w, 0), (0, 0)))[:, :l_ext]
        count = jnp.minimum(pos + 1, w).astype(jnp.float32)
        mean = (s - s_lag) / count[None, :, None]
        outs.append(mean - xg)
    pooled = jnp.stack(outs, axis=2)[:, lb:]
    y = jnp.einsum('nlgc,gcd->nlgd', pooled, w_pool.astype(jnp.float32))
    y = y.reshape(n, l, D_MODEL) * pool_scale.astype(jnp.float32)
    new_buf = ext[:, -POOL_BUF:].astype(buf.dtype)
    return y.astype(u.dtype), new_buf


def _cplx_affine_combine(e1, e2):
    a1r, a1i, b1r, b1i = e1
    a2r, a2i, b2r, b2i = e2
    ar = a2r * a1r - a2i * a1i
    ai = a2r * a1i + a2i * a1r
    br = a2r * b1r - a2i * b1i + b2r
    bi = a2r * b1i + a2i * b1r + b2i
    return (ar, ai, br, bi)


def s5_mixer(u, h0_re, h0_im, lam_re, lam_im, log_dt, b_re, b_im, c_re, c_im, d_skip):
    n, l, _ = u.shape
    uf = u.astype(jnp.float32).reshape(n, l, SSM_GROUPS, SSM_GROUP_DIM)
    dt = jnp.exp(log_dt.astype(jnp.float32))[:, None]
    lr = lam_re.astype(jnp.float32)
    li = lam_im.astype(jnp.float32)
    mag = jnp.exp(lr * dt)
    ang = li * dt
    abar_re = mag * jnp.cos(ang)
    abar_im = mag * jnp.sin(ang)
    den = lr * lr + li * li
    nr = abar_re - 1.0
    ni = abar_im
    k_re = ((nr * lr + ni * li) / den)[:, :, None]
    k_im = ((ni * lr - nr * li) / den)[:, :, None]
    br = b_re.astype(jnp.float32)
    bi = b_im.astype(jnp.float32)
    bbar_re = k_re * br - k_im * bi
    bbar_im = k_re * bi + k_im * br
    bu_re = jnp.einsum('nlgh,gph->nlgp', uf, bbar_re)
    bu_im = jnp.einsum('nlgh,gph->nlgp', uf, bbar_im)
    h0r = h0_re.astype(jnp.float32)
    h0i = h0_im.astype(jnp.float32)
    bu_re = bu_re.at[:, 0].add(abar_re * h0r - abar_im * h0i)
    bu_im = bu_im.at[:, 0].add(abar_re * h0i + abar_im * h0r)
    a_re = jnp.broadcast_to(abar_re, bu_re.shape)
    a_im = jnp.broadcast_to(abar_im, bu_im.shape)
    _, _, hr, hi = lax.associative_scan(_cplx_affine_combine, (a_re, a_im, bu_re, bu_im), axis=1)
    y = (jnp.einsum('nlgp,ghp->nlgh', hr, c_re.astype(jnp.float32))
         - jnp.einsum('nlgp,ghp->nlgh', hi, c_im.astype(jnp.float32))
         + d_skip.astype(jnp.float32).reshape(SSM_GROUPS, SSM_GROUP_DIM) * uf)
    return (y.reshape(n, l, SSM_WIDTH).astype(u.dtype),
            hr[:, -1].astype(h0_re.dtype), hi[:, -1].astype(h0_im.dtype))


def layer(x, p_l, pool_buf, h_re, h_im, pos_start,
          g_mix, w_in, w_pool, pool_scale, lam_re, lam_im, log_dt, b_re, b_im, c_re, c_im,
          d_skip, w_glu_v, w_glu_g, w_out, g_ff, w_ff1, w_ff2, g_ple, w_ple, w_ple_gate):
    h = rmsnorm(x, g_mix)
    z = h @ w_in
    s1 = POOL_WIDTH
    s2 = s1 + SSM_WIDTH
    s3 = s2 + D_MODEL
    u_a, u_b, gate_a, gate_b = z[..., :s1], z[..., s1:s2], z[..., s2:s3], z[..., s3:]
    branch_a, new_buf = pool_mixer(u_a, pool_buf, pos_start, w_pool, pool_scale)
    s, new_re, new_im = s5_mixer(u_b, h_re, h_im, lam_re, lam_im, log_dt, b_re, b_im,
                                 c_re, c_im, d_skip)
    gl = jax.nn.gelu(s)
    branch_b = (gl @ w_glu_v) * jax.nn.sigmoid(gl @ w_glu_g)
    merged = jax.nn.sigmoid(gate_a) * branch_a + jax.nn.sigmoid(gate_b) * branch_b
    x = x + merged @ w_out
    h2 = rmsnorm(x, g_ff)
    x = x + jnp.square(jax.nn.relu(h2 @ w_ff1)) @ w_ff2
    h3 = rmsnorm(x, g_ple)
    x = x + (p_l @ w_ple) * jax.nn.sigmoid(h3 @ w_ple_gate)
    return x, new_buf, new_re, new_im


def setup_inputs(seed: int = 0) -> dict:
    key = jax.random.key(seed)
    ks = jax.random.split(key, 32)
    f32 = jnp.float32
    nrm = lambda k, shape, s: jax.random.normal(k, shape, f32) * s
    gain = lambda k: 1.0 + 0.02 * jax.random.normal(k, (DEPTH, D_MODEL), f32)
    lam_im = (jnp.pi * jnp.arange(SSM_STATE, dtype=f32))[None, None, :] + nrm(ks[8], (DEPTH, SSM_GROUPS, SSM_STATE), 1e-3)
    return {
        'x_prompt': nrm(ks[0], (BATCH, SEQ, D_MODEL), 1.0),
        'x_sample': nrm(ks[1], (DEC_BATCH, DEC_SEQ, D_MODEL), 1.0),
        'p_prompt': nrm(ks[2], (DEPTH, BATCH, SEQ, PLE_DIM), 1.0),
        'p_sample': nrm(ks[3], (DEPTH, DEC_BATCH, DEC_SEQ, PLE_DIM), 1.0),
        'state_pool': nrm(ks[4], (DEPTH, DEC_BATCH, POOL_BUF, POOL_WIDTH), 1.0),
        'state_ssm_re': nrm(ks[5], (DEPTH, DEC_BATCH, SSM_GROUPS, SSM_STATE), 0.1),
        'state_ssm_im': nrm(ks[6], (DEPTH, DEC_BATCH, SSM_GROUPS, SSM_STATE), 0.1),
        'g_mix': gain(ks[7]),
        'w_in': nrm(ks[9], (DEPTH, D_MODEL, IN_WIDTH), D_MODEL ** -0.5),
        'w_pool': nrm(ks[10], (DEPTH, POOL_GROUPS, POOL_GROUP_DIM, POOL_OUT_GROUP_DIM), POOL_GROUP_DIM ** -0.5),
        'pool_scale': 1.0 + 0.1 * jax.random.normal(ks[11], (DEPTH, D_MODEL), f32),
        'lam_re': -0.5 + nrm(ks[12], (DEPTH, SSM_GROUPS, SSM_STATE), 1e-3),
        'lam_im': lam_im,
        'log_dt': jax.random.uniform(ks[13], (DEPTH, SSM_GROUPS), f32, float(np.log(DT_MIN)), float(np.log(DT_MAX))),
        'b_re': nrm(ks[14], (DEPTH, SSM_GROUPS, SSM_STATE, SSM_GROUP_DIM), (2 * SSM_GROUP_DIM) ** -0.5),
        'b_im': nrm(ks[15], (DEPTH, SSM_GROUPS, SSM_STATE, SSM_GROUP_DIM), (2 * SSM_GROUP_DIM) ** -0.5),
        'c_re': nrm(ks[16], (DEPTH, SSM_GROUPS, SSM_GROUP_DIM, SSM_STATE), SSM_STATE ** -0.5),
        'c_im': nrm(ks[17], (DEPTH, SSM_GROUPS, SSM_GROUP_DIM, SSM_STATE), SSM_STATE ** -0.5),
        'd_skip': nrm(ks[18], (DEPTH, SSM_WIDTH), 1.0),
        'w_glu_v': nrm(ks[19], (DEPTH, SSM_WIDTH, D_MODEL), SSM_WIDTH ** -0.5),
        'w_glu_g': nrm(ks[20], (DEPTH, SSM_WIDTH, D_MODEL), SSM_WIDTH ** -0.5),
        'w_out': nrm(ks[21], (DEPTH, D_MODEL, D_MODEL), D_MODEL ** -0.5),
        'g_ff': gain(ks[22]),
        'w_ff1': nrm(ks[23], (DEPTH, D_MODEL, D_FF), D_MODEL ** -0.5),
        'w_ff2': nrm(ks[24], (DEPTH, D_FF, D_MODEL), D_FF ** -0.5),
        'g_ple': gain(ks[25]),
        'w_ple': nrm(ks[26], (DEPTH, PLE_DIM, D_MODEL), PLE_DIM ** -0.5),
        'w_ple_gate': nrm(ks[27], (DEPTH, D_MODEL, D_MODEL), D_MODEL ** -0.5),
        'g_final': 1.0 + 0.02 * jax.random.normal(ks[28], (D_MODEL,), f32),
    }


def reference(x_prompt, x_sample, p_prompt, p_sample, state_pool, state_ssm_re, state_ssm_im,
              g_mix, w_in, w_pool, pool_scale, lam_re, lam_im, log_dt, b_re, b_im, c_re, c_im,
              d_skip, w_glu_v, w_glu_g, w_out, g_ff, w_ff1, w_ff2, g_ple, w_ple, w_ple_gate,
              g_final):
    yp = x_prompt
    ys = x_sample
    empty_buf = jnp.zeros((BATCH, 0, POOL_WIDTH), state_pool.dtype)
    zero_state = jnp.zeros((BATCH, SSM_GROUPS, SSM_STATE), state_ssm_re.dtype)
    pool_p, pool_s, re_p, im_p, re_s, im_s = [], [], [], [], [], []
    for l in range(DEPTH):
        lw = (g_mix[l], w_in[l], w_pool[l], pool_scale[l], lam_re[l], lam_im[l], log_dt[l],
              b_re[l], b_im[l], c_re[l], c_im[l], d_skip[l], w_glu_v[l], w_glu_g[l], w_out[l],
              g_ff[l], w_ff1[l], w_ff2[l], g_ple[l], w_ple[l], w_ple_gate[l])
        yp, bp, hrp, hip = layer(yp, p_prompt[l], empty_buf, zero_state, zero_state, 0, *lw)
        ys, bs, hrs, his = layer(ys, p_sample[l], state_pool[l], state_ssm_re[l],
                                 state_ssm_im[l], PAST_LEN, *lw)
        pool_p.append(bp)
        pool_s.append(bs)
        re_p.append(hrp)
        im_p.append(hip)
        re_s.append(hrs)
        im_s.append(his)
    y_prompt = rmsnorm(yp, g_final)
    y_sample = rmsnorm(ys, g_final)
    return (y_prompt, y_sample, jnp.stack(pool_p), jnp.stack(pool_s), jnp.stack(re_p),
            jnp.stack(im_p), jnp.stack(re_s), jnp.stack(im_s))
```

```python
import math
import numpy as np
import ml_dtypes
import concourse.bass as bass
import concourse.mybir as mybir
from concourse.bass_utils import run_bass_kernel_spmd

F32 = mybir.dt.float32
BF16 = mybir.dt.bfloat16
ALU = mybir.AluOpType
AF = mybir.ActivationFunctionType


class Buf:
    __slots__ = ("name", "last_w", "readers", "grp_deps", "grp_open")

    def __init__(self, name):
        self.name = name
        self.last_w = []
        self.readers = []
        self.grp_deps = []
        self.grp_open = False


class Op:
    __slots__ = ("eng", "fn", "deps", "is_dma", "sig", "sem", "val", "idx", "prev_dma")

    def __init__(self, eng, fn, deps, is_dma):
        self.eng = eng
        self.fn = fn
        self.deps = deps
        self.is_dma = is_dma
        self.sig = False
        self.sem = None
        self.val = 0
        self.idx = -1
        self.prev_dma = None


ENGS = ("pe", "act", "dve", "pool", "sp")
N_HW_SEMS = 32
N_SW_SEMS = 24
N_DMA_SEMS = N_HW_SEMS + N_SW_SEMS


class Prog:
    def __init__(self, nc):
        self.nc = nc
        self.ops = {e: [] for e in ENGS}
        self.all_ops = []
        self.n_dma = 0
        self.n_hw = 0
        self.n_sw = 0
        self.dma_last = [None] * N_DMA_SEMS
        self.dma_cnt = [0] * N_DMA_SEMS
        self.out_dmas = []

    def op(self, eng, fn, reads=(), writes=(), dma=False, is_output=False, group=False):
        deps = []
        for b in reads:
            deps.extend(b.last_w)
        for b in writes:
            if group and b.grp_open:
                deps.extend(b.grp_deps)
            else:
                deps.extend(b.last_w)
                deps.extend(b.readers)
        o = Op(eng, fn, deps, dma)
        if dma:
            if eng == "pool":
                slot = N_HW_SEMS + self.n_sw % N_SW_SEMS
                self.n_sw += 1
            else:
                slot = self.n_hw % N_HW_SEMS
                self.n_hw += 1
            self.n_dma += 1
            o.prev_dma = self.dma_last[slot]
            self.dma_last[slot] = o
            self.dma_cnt[slot] += 1
            o.sem = slot
            o.val = 16 * self.dma_cnt[slot]
            o.sig = True
            if is_output:
                self.out_dmas.append(o)
        for b in reads:
            b.readers.append(o)
            b.grp_open = False
        for b in writes:
            if group and b.grp_open:
                b.last_w.append(o)
            else:
                b.grp_deps = list(b.last_w) + list(b.readers)
                b.last_w = [o]
                b.readers = []
                b.grp_open = group
        o.idx = len(self.all_ops)
        self.all_ops.append(o)
        self.ops[eng].append(o)
        return o

    def pe(self, fn, reads=(), writes=()):
        return self.op("pe", fn, reads, writes)

    def act(self, fn, reads=(), writes=()):
        return self.op("act", fn, reads, writes)

    def dve(self, fn, reads=(), writes=()):
        return self.op("dve", fn, reads, writes)

    def pool(self, fn, reads=(), writes=()):
        return self.op("pool", fn, reads, writes)

    def dma(self, eng, out, in_, reads=(), writes=(), is_output=False, group=False, **kw):
        return self.op(eng, lambda e: e.dma_start(out=out, in_=in_, **kw), reads, writes,
                       dma=True, is_output=is_output, group=group)

    def emit(self, block_ctx_sems):
        nc = self.nc
        eng_sem = block_ctx_sems["eng"]
        dma_sem = block_ctx_sems["dma"]
        for o in self.all_ops:
            for d in o.deps:
                if not d.is_dma:
                    if d.eng == "pe" and o.eng == "pe" and not o.is_dma:
                        continue
                    d.sig = True
        for e in ENGS:
            c = 0
            for o in self.ops[e]:
                if o.is_dma:
                    continue
                if o.sig:
                    c += 1
                    o.val = c
                    o.sem = e
        final_waits = {}
        for o in self.out_dmas:
            final_waits[o.sem] = max(final_waits.get(o.sem, 0), o.val)

        def run_engine(ename, eng):
            seen = {}
            for o in self.ops[ename]:
                need = {}
                for d in o.deps:
                    if (not d.is_dma) and d.eng == "pe" and ename == "pe" and not o.is_dma:
                        continue
                    key = ("d", d.sem) if d.is_dma else ("e", d.sem)
                    if d.val > need.get(key, 0):
                        need[key] = d.val
                if o.is_dma and o.prev_dma is not None:
                    key = ("d", o.prev_dma.sem)
                    if o.prev_dma.val > need.get(key, 0):
                        need[key] = o.prev_dma.val
                for key, v in need.items():
                    if seen.get(key, 0) >= v:
                        continue
                    seen[key] = v
                    s = dma_sem[key[1]] if key[0] == "d" else eng_sem[key[1]]
                    eng.wait_ge(s, v)
                ins = o.fn(eng)
                if o.is_dma:
                    ins.then_inc(dma_sem[o.sem], 16)
                elif o.sig:
                    ins.then_inc(eng_sem[o.sem], 1)
            if ename == "sp":
                for s, v in final_waits.items():
                    if seen.get(("d", s), 0) < v:
                        eng.wait_ge(dma_sem[s], v)

        return run_engine


class Arena:
    def __init__(self, handle, nbytes):
        self.h = handle
        self.nbytes = nbytes
        self.top = 0
        self.live = []
        self.freed = []

    def alloc(self, name, shape, dt):
        esz = 4 if dt == F32 else 2
        n = 1
        for s in shape[1:]:
            n *= s
        size = (n * esz + 3) // 4 * 4
        off = self.top
        self.top += size
        assert self.top <= self.nbytes, ("SBUF arena overflow", name, self.top)
        v = self.h[0:shape[0], off // 4:(off + size) // 4]
        if dt != F32:
            v = v.bitcast(dt)[:, 0:n]
        if len(shape) == 3:
            v = v.rearrange("p (a b) -> p a b", b=shape[2])
        elif len(shape) == 4:
            v = v.rearrange("p (a b c) -> p a b c", b=shape[2], c=shape[3])
        elif len(shape) == 5:
            v = v.rearrange("p (a b c d) -> p a b c d", b=shape[2], c=shape[3], d=shape[4])
        buf = Buf(name)
        keep = []
        for (o, s, b) in self.freed:
            if o < off + size and off < o + s:
                buf.readers.extend(b.last_w)
                buf.readers.extend(b.readers)
            keep.append((o, s, b))
        self.freed = keep
        self.live.append((off, size, buf))
        return v, buf

    def sub_buf(self, name, base_off, rel_off, size):
        off = base_off + rel_off
        buf = Buf(name)
        for (o, s_, b) in self.freed:
            if o < off + size and off < o + s_:
                buf.readers.extend(b.last_w)
                buf.readers.extend(b.readers)
        return buf

    def mark(self):
        return self.top

    def release(self, mark):
        nl = []
        for (o, s, b) in self.live:
            if o >= mark:
                self.freed.append((o, s, b))
            else:
                nl.append((o, s, b))
        self.live = nl
        self.top = mark


def dram_ap(ap, offset, pattern):
    return bass.AP(ap.tensor, offset, pattern)


SEQ = 2048
DM = 1024
NSAMP = 16
TOK = 512
TOKB = 256
TCH = 8
ARENA_BYTES = 212000
GC1 = 2.0 * math.sqrt(2.0 / math.pi)


def build_nc():
    from contextlib import ExitStack
    nc = bass.Bass("TRN2", target_bir_lowering=False)
    D = {}

    def din(name, shape, dt=F32):
        D[name] = nc.dram_tensor(name, list(shape), dt, kind="ExternalInput").ap()

    def dout(name, shape):
        D[name] = nc.dram_tensor(name, list(shape), F32, kind="ExternalOutput").ap()

    din("x", [SEQ, DM]); din("xs", [NSAMP, DM]); din("p", [SEQ, 256]); din("ps", [NSAMP, 256])
    din("spool", [NSAMP, 15, 512]); din("sre", [NSAMP, 2048]); din("sim", [NSAMP, 2048])
    din("g_mix", [DM]); din("w_in", [DM, 3072]); din("w_pool", [4, 128, 256]); din("pool_scale", [DM])
    din("lam_re", [32, 64]); din("lam_im", [32, 64]); din("log_dt", [32])
    din("b_re", [32, 64, 16]); din("b_im", [32, 64, 16]); din("c_re", [32, 16, 64]); din("c_im", [32, 16, 64])
    din("d_skip", [512]); din("w_glu_v", [512, DM]); din("w_glu_g", [512, DM]); din("w_out", [DM, DM])
    din("g_ff", [DM]); din("w_ff1", [DM, 4096]); din("w_ff2", [4096, DM]); din("g_ple", [DM])
    din("w_ple", [256, DM]); din("w_ple_gate", [DM, DM]); din("g_final", [DM])
    din("identf", [128, 128]); din("poolcorr", [128, 4, 16])
    dout("y", [SEQ, DM]); dout("ys", [NSAMP, DM]); dout("pool_p", [15, 512]); dout("pool_s", [NSAMP, 15, 512])
    dout("re_p", [32, 64]); dout("im_p", [32, 64]); dout("re_s", [NSAMP, 2048]); dout("im_s", [NSAMP, 2048])
    x1d = nc.dram_tensor("x1_scratch", [SEQ + NSAMP, DM], F32).ap()

    P = Prog(nc)
    es = ExitStack()
    with es:
        arena_h = es.enter_context(nc.sbuf_tensor("arena", [128, ARENA_BYTES // 4], F32))
        psum = es.enter_context(nc.psum_tensor("psum", [128, 8, 512], F32))
        AR_ = Arena(arena_h, ARENA_BYTES)
        A = AR_.alloc
        pbank = [Buf("bank%d" % i) for i in range(8)]
        bank_ctr = [0]

        held = set()

        def bank():
            b = bank_ctr[0] % 8
            bank_ctr[0] += 1
            assert b not in held, ("PSUM bank handed out while a deferred reader is pending", b)
            return b

        def bank2():
            if bank_ctr[0] % 2:
                bank_ctr[0] += 1
            b = bank_ctr[0] % 8
            bank_ctr[0] += 2
            assert b not in held and (b + 1) not in held, ("PSUM bank handed out while a deferred reader is pending", b)
            return b

        def bank4():
            while bank_ctr[0] % 4:
                bank_ctr[0] += 1
            b = bank_ctr[0] % 8
            bank_ctr[0] += 4
            return b

        x1buf = [Buf("x1d%d" % i) for i in range(SEQ // 128 + 1)]
        outbuf = Buf("outs")

        identf, b_identf = A("identf", [128, 128], F32)
        P.dma("sp", identf, D["identf"], writes=[b_identf])
        identb, b_identb = A("identb", [128, 128], BF16)
        cols, b_cols = A("cols", [128, 5, 8], F32)
        stg, b_stg = A("stg", [64, 128], F32)

        def emit_cols():
            P.dve(lambda e: e.tensor_copy(out=identb, in_=identf), [b_identf], [b_identb])
            for i, nm in enumerate(["g_mix", "g_ff", "g_ple", "pool_scale"]):
                P.dma("sp", stg[8 * i:8 * i + 8, :], D[nm].rearrange("(c p) -> c p", p=128), writes=[b_stg], group=True)
            P.dma("sp", stg[32:36, :], D["d_skip"].rearrange("(c p) -> c p", p=128), writes=[b_stg], group=True)
            bk0 = bank()
            P.pe(lambda e: e.transpose(out=psum[:, bk0, 0:36], in_=stg[0:36, :], identity=identf[0:36, 0:36]), [b_stg, b_identf], [pbank[bk0]])
            P.dve(lambda e: e.tensor_copy(out=cols.rearrange("p a b -> p (a b)")[:, 0:36], in_=psum[:, bk0, 0:36]), [pbank[bk0]], [b_cols])
        mark0 = AR_.mark()

        W1, b_W1 = A("W1", [128, 4, 8, 2, 128], BF16)
        W3f, b_W3 = A("W3f", [128, 9, 2, 16, 32], BF16)
        W3 = W3f[:, 1:9]
        KB_, b_KB = A("Kblk", [128, 4, 8, 128], BF16)
        ER, b_ER = A("ER", [128, 16, 64], F32)
        EI, b_EI = A("EI", [128, 16, 64], F32)
        tb, b_tb = A("tb", [128, 12, 16], F32)
        poolcorr, b_pc = A("poolcorr", [128, 4, 16], F32)
        P.dma("sp", poolcorr, D["poolcorr"], writes=[b_pc])
        w_in, b_win = A("w_in", [128, 8, 3072], BF16)
        for k in range(8):
            P.dma("pool", w_in[:, k, :], D["w_in"][k * 128:(k + 1) * 128, :], writes=[b_win], group=True, max_dma_last_dim=4096)
        wgv, b_wgv = A("wgv", [128, 4, DM], BF16)
        wgg, b_wgg = A("wgg", [128, 4, DM], BF16)
        for k in range(4):
            P.dma("pool", wgv[:, k, :], D["w_glu_v"][k * 128:(k + 1) * 128, :], writes=[b_wgv], group=True, max_dma_last_dim=4096)
            P.dma("pool", wgg[:, k, :], D["w_glu_g"][k * 128:(k + 1) * 128, :], writes=[b_wgg], group=True, max_dma_last_dim=4096)
        wpool, b_wpool = A("wpool", [128, 4, 256], BF16)
        P.dma("pool", wpool, D["w_pool"].rearrange("g p n -> p g n"), writes=[b_wpool])
        markA = AR_.mark()

        prm, b_prm = A("prm", [128, 24, 16], F32)
        stgL, b_stgL = A("stgL", [32, 2, 2, 64], F32)
        ldtb, b_ldtb = A("ldtb", [128, 32], F32)
        PW, b_PW = A("PW", [128, 2, 9, 16], F32)
        Bb, b_Bb = A("Bb", [128, 2, 16, 32], F32)
        Cb, b_Cb = A("Cb", [128, 2, 16, 32], F32)
        Cn, b_Cn = A("Cn", [128, 2, 4, 2, 64], F32)
        KBk, b_KBk = A("KBk", [128, 2, 16, 32], F32)
        T1, b_T1 = A("T1s", [128, 4, 16, 32], F32)
        T2, b_T2 = A("T2s", [128, 4, 16, 32], F32)
        U1, b_U1 = A("U1s", [128, 2, 16, 32], F32)
        U2, b_U2 = A("U2s", [128, 2, 16, 32], F32)
        ABa, b_ABa = A("ABa", [128, 2, 8, 16, 32], BF16)
        Bpad, b_Bpad = A("Bpad", [128, 2, 16, 128], BF16)

        def V(fn, r, w):
            return P.dve(fn, r, w)

        def prmv(i):
            return prm[:, i, :]

        S_ = [b_prm]
        V(lambda e: e.memset(Bb, 0.0), [], [b_Bb])
        V(lambda e: e.memset(Cb, 0.0), [], [b_Cb])
        V(lambda e: e.memset(Bpad, 0.0), [], [b_Bpad])
        for ti, nm in enumerate(["lam_re", "lam_im"]):
            for dup in range(2):
                P.dma("sp", stgL[:, ti, dup, :], D[nm], writes=[b_stgL], group=True)
        P.dma("sp", ldtb, D["log_dt"].partition_broadcast(128), writes=[b_ldtb])
        for ri, nm in enumerate(["b_re", "b_im"]):
            for gl in range(2):
                P.dma("act", Bb[gl * 64:(gl + 1) * 64, ri, :, gl * 16:(gl + 1) * 16],
                      dram_ap(D[nm], gl * 1024, [[16, 64], [2048, 16], [1, 16]]), writes=[b_Bb], group=True)
        for ri, nm in enumerate(["c_re", "c_im"]):
            for dup in range(2):
                P.dma("sp", Cn[:, ri, :, dup, :],
                      D[nm].rearrange("g h p -> (g h) p").rearrange("(c r) p -> r c p", r=128), writes=[b_Cn], group=True)
        for ti in range(2):
            bkL = bank()
            P.pe(lambda e, ti=ti, bkL=bkL: e.transpose(out=psum[:, bkL, 0:32], in_=stgL[:, ti, :, :].rearrange("p a b -> p (a b)"),
                                                       identity=identf[0:32, 0:32]), [b_stgL, b_identf], [pbank[bkL]])
            for gl in range(2):
                P.dve(lambda e, ti=ti, gl=gl, bkL=bkL: e.tensor_copy(out=prm[gl * 64:(gl + 1) * 64, ti, :], in_=psum[gl * 64:(gl + 1) * 64, bkL, gl:32:2]),
                      [pbank[bkL]], [b_prm])
        for gl in range(2):
            P.dve(lambda e, gl=gl: e.tensor_copy(out=prm[gl * 64:(gl + 1) * 64, 2, :], in_=ldtb[gl * 64:(gl + 1) * 64, gl:32:2]), [b_ldtb], [b_prm])
        def tt(out, a, b, op, r=S_, w=S_):
            return V(lambda e: e.tensor_tensor(out=out, in0=a, in1=b, op=op), r, w)

        def act(out, in_, func, r=S_, w=S_, **kw):
            return P.act(lambda e: e.activation(out=out, in_=in_, func=func, **kw), r, w)

        act(prmv(2), prmv(2), AF.Exp)
        tt(prmv(7), prmv(0), prmv(2), ALU.mult)
        act(prmv(3), prmv(7), AF.Exp)
        tt(prmv(4), prmv(1), prmv(2), ALU.mult)
        halfpi, b_hp = A("halfpi", [128, 1], F32)
        V(lambda e: e.memset(halfpi, math.pi / 2), [], [b_hp])
        act(prmv(6), prmv(4), AF.Sin, scale=1.0 / 16)
        act(prmv(5), prmv(4), AF.Sin, r=S_ + [b_hp], scale=1.0 / 16, bias=halfpi)
        for _ in range(4):
            tt(prmv(7), prmv(5), prmv(5), ALU.mult)
            tt(prmv(8), prmv(6), prmv(6), ALU.mult)
            V(lambda e: e.scalar_tensor_tensor(out=prmv(6), in0=prmv(5), scalar=2.0, in1=prmv(6), op0=ALU.mult, op1=ALU.mult), S_, S_)
            tt(prmv(5), prmv(7), prmv(8), ALU.subtract)
        tt(prmv(9), prmv(3), prmv(5), ALU.mult)
        tt(prmv(10), prmv(3), prmv(6), ALU.mult)
        tt(prmv(7), prmv(0), prmv(0), ALU.mult)
        tt(prmv(8), prmv(1), prmv(1), ALU.mult)
        tt(prmv(11), prmv(7), prmv(8), ALU.add)
        V(lambda e: e.reciprocal(out=prmv(11), in_=prmv(11)), S_, S_)
        V(lambda e: e.tensor_scalar_add(out=prmv(12), in0=prmv(9), scalar1=-1.0), S_, S_)
        tt(prmv(7), prmv(12), prmv(0), ALU.mult)
        tt(prmv(8), prmv(10), prmv(1), ALU.mult)
        tt(prmv(7), prmv(7), prmv(8), ALU.add)
        tt(prmv(13), prmv(7), prmv(11), ALU.mult)
        tt(prmv(7), prmv(10), prmv(0), ALU.mult)
        tt(prmv(8), prmv(12), prmv(1), ALU.mult)
        tt(prmv(7), prmv(7), prmv(8), ALU.subtract)
        tt(prmv(14), prmv(7), prmv(11), ALU.mult)
        SP_ = S_ + [b_PW]
        V(lambda e: e.memset(PW[:, 0, 0, :], 1.0), [], [b_PW])
        V(lambda e: e.memset(PW[:, 1, 0, :], 0.0), [], [b_PW])
        for m in range(8):
            tt(prmv(7), PW[:, 0, m, :], prmv(9), ALU.mult, SP_, SP_)
            tt(prmv(8), PW[:, 1, m, :], prmv(10), ALU.mult, SP_, SP_)
            tt(PW[:, 0, m + 1, :], prmv(7), prmv(8), ALU.subtract, SP_, SP_)
            tt(prmv(7), PW[:, 0, m, :], prmv(10), ALU.mult, SP_, SP_)
            tt(prmv(8), PW[:, 1, m, :], prmv(9), ALU.mult, SP_, SP_)
            tt(PW[:, 1, m + 1, :], prmv(7), prmv(8), ALU.add, SP_, SP_)
        ST_ = SP_ + [b_tb]
        V(lambda e: e.tensor_copy(out=tb[:, 0, :], in_=PW[:, 0, 1, :]), ST_, ST_)
        V(lambda e: e.tensor_copy(out=tb[:, 1, :], in_=PW[:, 1, 1, :]), ST_, ST_)
        tt(prmv(7), PW[:, 0, 8, :], PW[:, 0, 8, :], ALU.mult, ST_, ST_)
        tt(prmv(8), PW[:, 1, 8, :], PW[:, 1, 8, :], ALU.mult, ST_, ST_)
        tt(prmv(7), prmv(7), prmv(8), ALU.add, ST_, ST_)
        act(tb[:, 4, :], prmv(7), AF.Sqrt, r=ST_, w=ST_)
        V(lambda e: e.reciprocal(out=prmv(8), in_=tb[:, 4, :]), ST_, ST_)
        tt(tb[:, 2, :], PW[:, 0, 8, :], prmv(8), ALU.mult, ST_, ST_)
        tt(tb[:, 3, :], PW[:, 1, 8, :], prmv(8), ALU.mult, ST_, ST_)
        V(lambda e: e.memset(tb[:, 5:7, :], 0.0), ST_, ST_)
        for ri in range(2):
            bk = bank()
            for c4 in range(4):
                P.pe(lambda e, ri=ri, c4=c4, bk=bk: e.transpose(out=psum[:, bk, c4 * 128:(c4 + 1) * 128],
                                                                in_=Cn[:, ri, c4, :, :].rearrange("p a b -> p (a b)"), identity=identf),
                     [b_Cn, b_identf], [pbank[bk]])
            pv = psum[:, bk, :].rearrange("p (c q g h) -> p c q g h", c=4, q=4, g=2)
            for c4 in range(4):
                V(lambda e, ri=ri, c4=c4, pv=pv: e.tensor_copy(out=Cb[0:64, ri, 4 * c4:4 * c4 + 4, 0:16], in_=pv[0:64, c4, :, 0, :]),
                  [pbank[bk]], [b_Cb])
                V(lambda e, ri=ri, c4=c4, pv=pv: e.tensor_copy(out=Cb[64:128, ri, 4 * c4:4 * c4 + 4, 16:32], in_=pv[64:128, c4, :, 1, :]),
                  [pbank[bk]], [b_Cb])

        emit_cols()
        SB_ = S_ + [b_Bb, b_KBk, b_T1, b_T2]
        t1k = T1[:, 0]; t2k = T2[:, 0]

        def bc(v):
            return v.unsqueeze(2).broadcast_to([128, 16, 32])

        tt(t1k, Bb[:, 0], bc(prmv(13)), ALU.mult, SB_, SB_)
        tt(t2k, Bb[:, 1], bc(prmv(14)), ALU.mult, SB_, SB_)
        tt(KBk[:, 0], t1k, t2k, ALU.subtract, SB_, SB_)
        tt(t1k, Bb[:, 0], bc(prmv(14)), ALU.mult, SB_, SB_)
        tt(t2k, Bb[:, 1], bc(prmv(13)), ALU.mult, SB_, SB_)
        tt(KBk[:, 1], t1k, t2k, ALU.add, SB_, SB_)

        def cmul_all(eng, out_re, out_im, src, b_src, m0, nm, ta, tb_, b_ta, b_tb_, b_out, neg_im):
            sre = src[:, 0].unsqueeze(1).broadcast_to([128, nm, 16, 32])
            sim = src[:, 1].unsqueeze(1).broadcast_to([128, nm, 16, 32])
            pr = PW[:, 0, m0:m0 + nm, :].unsqueeze(3).broadcast_to([128, nm, 16, 32])
            pi_ = PW[:, 1, m0:m0 + nm, :].unsqueeze(3).broadcast_to([128, nm, 16, 32])
            ta = ta[:, 0:nm]; tb_ = tb_[:, 0:nm]
            rd = [b_src, b_PW, b_ta, b_tb_]
            P.op(eng, lambda e: e.tensor_tensor(out=ta, in0=sre, in1=pr, op=ALU.mult), rd, [b_ta])
            P.op(eng, lambda e: e.tensor_tensor(out=tb_, in0=sim, in1=pi_, op=ALU.mult), rd, [b_tb_])
            P.op(eng, lambda e: e.tensor_tensor(out=out_re, in0=ta, in1=tb_, op=ALU.subtract), [b_ta, b_tb_], [b_out])
            P.op(eng, lambda e: e.tensor_tensor(out=ta, in0=sre, in1=pi_, op=ALU.mult), rd, [b_ta])
            P.op(eng, lambda e: e.tensor_tensor(out=tb_, in0=sim, in1=pr, op=ALU.mult), rd, [b_tb_])
            if neg_im:
                P.op(eng, lambda e: e.tensor_tensor(out=ta, in0=ta, in1=tb_, op=ALU.add), [b_ta, b_tb_], [b_ta])
                P.op(eng, lambda e: e.tensor_scalar_mul(out=out_im, in0=ta, scalar1=-1.0), [b_ta], [b_out])
            else:
                P.op(eng, lambda e: e.tensor_tensor(out=out_im, in0=ta, in1=tb_, op=ALU.add), [b_ta, b_tb_], [b_out])

        for m0 in (0, 4):
            cmul_all("dve", ABa[:, 0, m0:m0 + 4], ABa[:, 1, m0:m0 + 4], KBk, b_KBk, m0, 4, T1, T2, b_T1, b_T2, b_ABa, False)
        for (m0, nm) in ((0, 2), (2, 2), (4, 2), (6, 2), (8, 1)):
            cmul_all("dve", W3f[:, m0:m0 + nm, 0], W3f[:, m0:m0 + nm, 1], Cb, b_Cb, m0, nm, U1, U2, b_U1, b_U2, b_W3, True)
        for ri in range(2):
            for ql in range(4):
                V(lambda e, ri=ri, ql=ql: e.tensor_copy(out=Bpad[:, ri, ql::4, 32 * ql:32 * ql + 32], in_=ABa[:, ri, 0, ql::4, :]),
                  [b_ABa], [b_Bpad])
        for m in range(8):
            bk = bank()
            pbf = psum[:, bk, :].bitcast(BF16)
            for ri in range(2):
                for cc in range(4):
                    P.pe(lambda e, ri=ri, cc=cc, m=m, pbf=pbf: e.transpose(out=pbf[:, (ri * 4 + cc) * 128:(ri * 4 + cc + 1) * 128],
                                                                          in_=ABa[:, ri, m, 4 * cc:4 * cc + 4, :].rearrange("p a b -> p (a b)"), identity=identb),
                         [b_ABa, b_identb], [pbank[bk]])
            if m % 2 == 0:
                P.act(lambda e, m=m, pbf=pbf: e.activation(out=W1[:, :, m, :, :], in_=pbf.rearrange("p (r c n) -> p c r n", r=2, c=4), func=AF.Copy),
                      [pbank[bk]], [b_W1])
            else:
                V(lambda e, m=m, pbf=pbf: e.tensor_copy(out=W1[:, :, m, :, :], in_=pbf.rearrange("p (r c n) -> p c r n", r=2, c=4)),
                  [pbank[bk]], [b_W1])
        for cc in range(4):
            for half in range(2):
                bk = bank()
                for t4 in range(4):
                    tau = half * 4 + t4
                    for ql in range(4):
                        for ri in range(2):
                            P.pe(lambda e, cc=cc, tau=tau, ql=ql, ri=ri, bk=bk, t4=t4: e.matmul(
                                out=psum[:, bk, t4 * 128 + 32 * ql:t4 * 128 + 32 * ql + 32],
                                lhsT=Bpad[:, ri, 4 * cc + ql, :], rhs=W3f[:, tau, ri, 4 * cc + ql, :],
                                start=(ri == 0), stop=(ri == 1)), [b_Bpad, b_W3], [pbank[bk]])
                if half == 0:
                    V(lambda e, cc=cc, bk=bk: e.scalar_tensor_tensor(out=KB_[:, cc, 0, :], in0=identf, scalar=cols[:, 4, cc:cc + 1],
                                                                     in1=psum[:, bk, 0:128], op0=ALU.mult, op1=ALU.add),
                      [pbank[bk], b_identf, b_cols], [b_KB])
                    V(lambda e, cc=cc, bk=bk: e.tensor_copy(out=KB_[:, cc, 1:4, :], in_=psum[:, bk, 128:512].rearrange("p (t n) -> p t n", n=128)),
                      [pbank[bk]], [b_KB])
                else:
                    V(lambda e, cc=cc, bk=bk: e.tensor_copy(out=KB_[:, cc, 4:8, :], in_=psum[:, bk, :].rearrange("p (t n) -> p t n", n=128)),
                      [pbank[bk]], [b_KB])
        E1, b_E1 = A("E1s", [128, 16, 32], F32)
        E2, b_E2 = A("E2s", [128, 16, 32], F32)
        SE_ = [b_prm, b_tb, b_ER, b_EI, b_E1, b_E2]

        def pt(out, a, b, op):
            return P.pool(lambda e: e.tensor_tensor(out=out, in0=a, in1=b, op=op), SE_, SE_)

        P.pool(lambda e: e.memset(ER[:, :, 0:1], 1.0), [], [b_ER])
        P.pool(lambda e: e.memset(EI[:, :, 0:1], 0.0), [], [b_EI])
        P.pool(lambda e: e.tensor_copy(out=prm[:, 15:17, :], in_=tb[:, 2:4, :]), SE_, SE_)
        for k in range(6):
            n = 1 << k
            pr = prm[:, 15, :].unsqueeze(2).broadcast_to([128, 16, n])
            pi_ = prm[:, 16, :].unsqueeze(2).broadcast_to([128, 16, n])
            pt(E1[:, :, 0:n], ER[:, :, 0:n], pr, ALU.mult)
            pt(E2[:, :, 0:n], EI[:, :, 0:n], pi_, ALU.mult)
            pt(ER[:, :, n:2 * n], E1[:, :, 0:n], E2[:, :, 0:n], ALU.subtract)
            pt(E1[:, :, 0:n], ER[:, :, 0:n], pi_, ALU.mult)
            pt(E2[:, :, 0:n], EI[:, :, 0:n], pr, ALU.mult)
            pt(EI[:, :, n:2 * n], E1[:, :, 0:n], E2[:, :, 0:n], ALU.add)
            if k < 5:
                pt(prmv(17), prmv(15), prmv(15), ALU.mult)
                pt(prmv(18), prmv(16), prmv(16), ALU.mult)
                pt(prmv(16), prmv(15), prmv(16), ALU.mult)
                P.pool(lambda e: e.tensor_scalar_mul(out=prmv(16), in0=prmv(16), scalar1=2.0), SE_, SE_)
                pt(prmv(15), prmv(17), prmv(18), ALU.subtract)
        AR_.release(markA)
        ctx = dict(nc=nc, P=P, D=D, A=A, AR_=AR_, psum=psum, pbank=pbank, bank=bank, bank2=bank2, bank4=bank4, held=held,
                   identf=identf, b_identf=b_identf, identb=identb, b_identb=b_identb, cols=cols, b_cols=b_cols,
                   W1=W1, b_W1=b_W1, W3=W3, b_W3=b_W3, KB_=KB_, b_KB=b_KB, ER=ER, b_ER=b_ER, EI=EI, b_EI=b_EI,
                   tb=tb, b_tb=b_tb, poolcorr=poolcorr, b_pc=b_pc, w_in=w_in, b_win=b_win, wgv=wgv, b_wgv=b_wgv,
                   wgg=wgg, b_wgg=b_wgg, wpool=wpool, b_wpool=b_wpool, x1d=x1d, x1buf=x1buf, outbuf=outbuf,
                   mark0=mark0, markA=markA)
        phase_a(ctx)
        AR_.release(mark0)
        phase_b(ctx)

        sems = {}
        sems["eng"] = {e: es.enter_context(nc.semaphore("s_" + e)) for e in ENGS}
        sems["dma"] = [es.enter_context(nc.semaphore("d%d" % i)) for i in range(N_DMA_SEMS)]
        block = es.enter_context(nc.Block())
        run = P.emit(sems)

        @block.tensor
        def _(e):
            run("pe", e)

        @block.scalar
        def _(e):
            run("act", e)

        @block.vector
        def _(e):
            run("dve", e)

        @block.gpsimd
        def _(e):
            run("pool", e)

        @block.sync
        def _(e):
            run("sp", e)
    return nc


def _norm_front(c, xt, b_xt, rows, slot):
    P = c["P"]
    xn, b_xn, ss, b_ss = slot
    P.act(lambda e: e.activation(out=xn[0:rows, :], in_=xt, func=AF.Square, accum_out=ss[0:rows, 0:1]), [b_xt], [b_xn, b_ss])
    P.dve(lambda e: e.tensor_scalar(out=ss[0:rows, 1:2], in0=ss[0:rows, 0:1], scalar1=1.0 / DM, scalar2=1e-6, op0=ALU.mult, op1=ALU.add), [b_ss], [b_ss])
    P.act(lambda e: e.activation(out=ss[0:rows, 1:2], in_=ss[0:rows, 1:2], func=AF.Sqrt), [b_ss], [b_ss])
    P.dve(lambda e: e.reciprocal(out=ss[0:rows, 2:3], in_=ss[0:rows, 1:2]), [b_ss], [b_ss])
    P.act(lambda e: e.activation(out=xn[0:rows, :], in_=xt, func=AF.Copy, scale=ss[0:rows, 2:3]), [b_xt, b_ss], [b_xn])


def _norm_back(c, rows, gi, hT, b_hT, col0, slot):
    P = c["P"]; psum = c["psum"]; pbank = c["pbank"]; cols = c["cols"]; identb = c["identb"]
    xn, b_xn, ss, b_ss = slot
    bk = c["bank"]()
    pbf = psum[:, bk, :].bitcast(BF16)
    for k in range(8):
        P.pe(lambda e, k=k: e.transpose(out=pbf[:, k * 128:k * 128 + rows], in_=xn[0:rows, k * 128:(k + 1) * 128],
                                        identity=identb[0:rows, 0:rows]), [b_xn, c["b_identb"]], [pbank[bk]])
    P.dve(lambda e: e.tensor_tensor(out=hT[:, 0:8, col0:col0 + rows],
                                    in0=pbf.rearrange("p (k n) -> p k n", n=128)[:, :, 0:rows],
                                    in1=cols[:, gi, 0:8].unsqueeze(2).broadcast_to([128, 8, rows]), op=ALU.mult),
          [pbank[bk], c["b_cols"]], [b_hT])


def _norm_T(c, xt, b_xt, rows, gi, hT, b_hT, col0, slot):
    _norm_front(c, xt, b_xt, rows, slot)
    _norm_back(c, rows, gi, hT, b_hT, col0, slot)


def phase_a(c):
    P = c["P"]; A = c["A"]; D = c["D"]; psum = c["psum"]; pbank = c["pbank"]; bank = c["bank"]; bank2 = c["bank2"]
    cols = c["cols"]; b_cols = c["b_cols"]; identf = c["identf"]; b_identf = c["b_identf"]
    W1 = c["W1"]; b_W1 = c["b_W1"]; W3 = c["W3"]; b_W3 = c["b_W3"]; KB_ = c["KB_"]; b_KB = c["b_KB"]
    ER = c["ER"]; EI = c["EI"]; b_ER = c["b_ER"]; b_EI = c["b_EI"]; tb = c["tb"]; b_tb = c["b_tb"]
    w_in = c["w_in"]; b_win = c["b_win"]; wgv = c["wgv"]; wgg = c["wgg"]; b_wgv = c["b_wgv"]; b_wgg = c["b_wgg"]
    wpool = c["wpool"]; b_wpool = c["b_wpool"]; x1d = c["x1d"]; x1buf = c["x1buf"]; outbuf = c["outbuf"]
    w_out, b_wout = A("w_out", [128, 8, DM], BF16)
    for k in range(8):
        P.dma("pool", w_out[:, k, :], D["w_out"][k * 128:(k + 1) * 128, :], writes=[b_wout], group=True, max_dma_last_dim=4096)
    xts = [A("xtA%d" % i, [128, DM], F32) for i in range(2)]
    slots = []
    for i in range(2):
        xn_, b_xn_ = A("xnA%d" % i, [128, DM], BF16)
        ss_, b_ss_ = A("ssA%d" % i, [128, 4], F32)
        slots.append((xn_, b_xn_, ss_, b_ss_))
    nslot = [0]
    hT, b_hT = A("hTA", [128, 8, TOK], BF16)
    L = 15 + TOK
    ua, b_ua = A("ua", [128, 4, L], F32)
    pa, b_pa = A("pa", [128, L], F32)
    pb, b_pb = A("pb", [128, L], F32)
    pooled, b_pooled = A("pooled", [128, 4, TOK], BF16)
    ub, b_ub = A("ub", [128, 4, TOK], BF16)
    gl, b_gl = A("gl", [128, 4, TOK], BF16)
    mgf, b_mg = A("mg", [128, 4 * TOK], F32)
    mg = mgf.bitcast(BF16).rearrange("p (k n) -> p k n", n=TOK)
    sblk, b_sblk = A("sblk", [128, 5 * TOK], F32)
    _sub = []
    _soff = c["AR_"].live[-1][0]
    for i in range(5):
        bb = Buf("sblk%d" % i)
        bb.readers = list(b_sblk.readers)
        _sub.append((sblk[:, i * TOK:(i + 1) * TOK], bb))
        c["AR_"].live.append((_soff + i * TOK * 4, TOK * 4, bb))
    (sa, b_sa), (sg, b_sg), (sb, b_sb), (sa2, b_sa2), (sg2, b_sg2) = _sub
    rslots = [(sblk[:, 0:2 * TOK], [b_sa, b_sg]), (sblk[:, 2 * TOK:4 * TOK], [b_sb, b_sa2])]
    rcnt = [0]
    sb2, b_sb2 = pa[:, 0:TOK], b_pa
    mbufs = [(sa, b_sa, sg, b_sg, sb, b_sb), (sa2, b_sa2, sg2, b_sg2, sb2, b_sb2)]
    Gt, b_Gt = A("Gt", [128, 2, 4, 64], F32)
    Tm, b_Tm = A("Tm", [128, 2, 4, 64], F32)
    St, b_St = A("St", [128, 2, 4, 64], F32)
    SPb, b_SPb = A("SPb", [128, 2, 16, 65], BF16)
    osm, b_osm = mgf[0:16, :], b_mg
    NCH = TOK // TCH
    xcnt = [0]

    def load_x(src):
        i = xcnt[0] % 2
        xcnt[0] += 1
        xt, b_xt = xts[i]
        return xt, b_xt

    fslot = {}

    def front_only(key, src_rows, rows):
        xt, b_xt = load_x(None)
        P.dma("pool", xt[0:rows, :], src_rows, writes=[b_xt])
        slot = slots[nslot[0] % 2]
        nslot[0] += 1
        _norm_front(c, xt[0:rows, :], b_xt, rows, slot)
        fslot[key] = slot

    def back_only(key, rows, col0):
        _norm_back(c, rows, 0, hT, b_hT, col0, fslot.pop(key))

    def dense_front(src_rows, rows, col0):
        front_only(("tmp", col0), src_rows, rows)
        back_only(("tmp", col0), rows, col0)

    def win_uaub(ntok):
        for oc in range(8):
            bk = bank()
            for k in range(8):
                P.pe(lambda e, oc=oc, k=k, bk=bk: e.matmul(out=psum[:, bk, 0:ntok], lhsT=w_in[:, k, oc * 128:(oc + 1) * 128],
                                                           rhs=hT[:, k, 0:ntok], start=(k == 0), stop=(k == 7)),
                     [b_win, b_hT], [pbank[bk]])
            if oc < 4:
                P.act(lambda e, oc=oc, bk=bk: e.activation(out=ua[:, oc, 15:15 + ntok], in_=psum[:, bk, 0:ntok], func=AF.Copy),
                      [pbank[bk]], [b_ua])
            else:
                if ntok == TOK:
                    P.act(lambda e, oc=oc, bk=bk: e.activation(out=ub[:, oc - 4, :].rearrange("p (i c) -> p i c", i=TCH),
                                                               in_=psum[:, bk, :].rearrange("p (c i) -> p i c", i=TCH), func=AF.Copy),
                          [pbank[bk]], [b_ub])
                else:
                    P.act(lambda e, oc=oc, bk=bk: e.activation(out=ub[:, oc - 4, 0:ntok], in_=psum[:, bk, 0:ntok], func=AF.Copy),
                          [pbank[bk]], [b_ub])

    half_ctr = [0, 0]

    def bank_other(bkG):
        base = (bkG + 4) % 8
        b = base + half_ctr[0] % 4
        half_ctr[0] += 1
        return b

    def gate_b_block(dc, ntok, bank_fn):
        sa, b_sa, sg, b_sg, sb, b_sb = mbufs[dc % 2]
        bk = bank_fn()
        for k in range(8):
            P.pe(lambda e, dc=dc, k=k, bk=bk: e.matmul(out=psum[:, bk, 0:ntok], lhsT=w_in[:, k, 2048 + dc * 128:2048 + (dc + 1) * 128],
                                                       rhs=hT[:, k, 0:ntok], start=(k == 0), stop=(k == 7)), [b_win, b_hT], [pbank[bk]])
        P.act(lambda e, bk=bk, sb=sb: e.activation(out=sb[:, 0:ntok], in_=psum[:, bk, 0:ntok], func=AF.Sigmoid), [pbank[bk]], [b_sb])

    def gate_a_prepass(ntok, bkG, dcs=range(8)):
        for dc in dcs:
            sa, b_sa, sg, b_sg, sb, b_sb = mbufs[dc % 2]
            bk = bank_other(bkG)
            for k in range(8):
                P.pe(lambda e, dc=dc, k=k, bk=bk: e.matmul(out=psum[:, bk, 0:ntok], lhsT=w_in[:, k, 1024 + dc * 128:1024 + (dc + 1) * 128],
                                                           rhs=hT[:, k, 0:ntok], start=(k == 0), stop=(k == 7)), [b_win, b_hT], [pbank[bk]])
            P.act(lambda e, bk=bk, sa=sa: e.activation(out=sa[:, 0:ntok], in_=psum[:, bk, 0:ntok], func=AF.Sigmoid), [pbank[bk]], [b_sa])
            bk = bank_other(bkG)
            P.pe(lambda e, dc=dc, bk=bk: e.matmul(out=psum[:, bk, 0:ntok], lhsT=wpool[:, dc // 2, (dc % 2) * 128:(dc % 2) * 128 + 128],
                                                  rhs=pooled[:, dc // 2, 0:ntok], start=True, stop=True), [b_wpool, b_pooled], [pbank[bk]])
            P.dve(lambda e, dc=dc, bk=bk, sa=sa: e.scalar_tensor_tensor(out=mg[:, dc, 0:ntok], in0=psum[:, bk, 0:ntok], scalar=cols[:, 3, dc:dc + 1],
                                                                        in1=sa[:, 0:ntok], op0=ALU.mult, op1=ALU.mult), [pbank[bk], b_cols, b_sa], [b_mg])

    def merge_and_out(ntok, subtiles, x_rows_of, x1_rows_of, x1b_of, prefetch=None, pre=False, gb_done=()):
        nx = prefetch
        if nx is not None:
            nx["f01"]()
        for dc in range(8):
            sa, b_sa, sg, b_sg, sb, b_sb = mbufs[dc % 2]
            if not pre:
                bk = bank()
                for k in range(8):
                    P.pe(lambda e, dc=dc, k=k, bk=bk: e.matmul(out=psum[:, bk, 0:ntok], lhsT=w_in[:, k, 1024 + dc * 128:1024 + (dc + 1) * 128],
                                                               rhs=hT[:, k, 0:ntok], start=(k == 0), stop=(k == 7)), [b_win, b_hT], [pbank[bk]])
                P.act(lambda e, bk=bk, sa=sa: e.activation(out=sa[:, 0:ntok], in_=psum[:, bk, 0:ntok], func=AF.Sigmoid), [pbank[bk]], [b_sa])
                bk = bank()
                P.pe(lambda e, dc=dc, bk=bk: e.matmul(out=psum[:, bk, 0:ntok], lhsT=wpool[:, dc // 2, (dc % 2) * 128:(dc % 2) * 128 + 128],
                                                      rhs=pooled[:, dc // 2, 0:ntok], start=True, stop=True), [b_wpool, b_pooled], [pbank[bk]])
                P.dve(lambda e, dc=dc, bk=bk, sa=sa: e.scalar_tensor_tensor(out=sa[:, 0:ntok], in0=psum[:, bk, 0:ntok], scalar=cols[:, 3, dc:dc + 1],
                                                                     in1=sa[:, 0:ntok], op0=ALU.mult, op1=ALU.mult), [pbank[bk], b_cols, b_sa], [b_sa])
            if dc not in gb_done:
                gate_b_block(dc, ntok, bank)
            bk = bank()
            for k in range(4):
                P.pe(lambda e, dc=dc, k=k, bk=bk: e.matmul(out=psum[:, bk, 0:ntok], lhsT=wgg[:, k, dc * 128:(dc + 1) * 128],
                                                           rhs=gl[:, k, 0:ntok], start=(k == 0), stop=(k == 3)), [b_wgg, b_gl], [pbank[bk]])
            P.act(lambda e, bk=bk, sg=sg: e.activation(out=sg[:, 0:ntok], in_=psum[:, bk, 0:ntok], func=AF.Sigmoid), [pbank[bk]], [b_sg])
            bk = bank()
            for k in range(4):
                P.pe(lambda e, dc=dc, k=k, bk=bk: e.matmul(out=psum[:, bk, 0:ntok], lhsT=wgv[:, k, dc * 128:(dc + 1) * 128],
                                                           rhs=gl[:, k, 0:ntok], start=(k == 0), stop=(k == 3)), [b_wgv, b_gl], [pbank[bk]])
            P.dve(lambda e, bk=bk, sg=sg: e.tensor_tensor(out=sg[:, 0:ntok], in0=psum[:, bk, 0:ntok], in1=sg[:, 0:ntok], op=ALU.mult),
                  [pbank[bk], b_sg], [b_sg])
            P.dve(lambda e, sg=sg, sb=sb: e.tensor_tensor(out=sg[:, 0:ntok], in0=sg[:, 0:ntok], in1=sb[:, 0:ntok], op=ALU.mult), [b_sg, b_sb], [b_sg])
            if pre:
                P.dve(lambda e, dc=dc, sg=sg: e.tensor_tensor(out=mg[:, dc, 0:ntok], in0=mg[:, dc, 0:ntok], in1=sg[:, 0:ntok], op=ALU.add), [b_mg, b_sg], [b_mg])
            else:
                P.dve(lambda e, dc=dc, sa=sa, sg=sg: e.tensor_tensor(out=mg[:, dc, 0:ntok], in0=sa[:, 0:ntok], in1=sg[:, 0:ntok], op=ALU.add), [b_sa, b_sg], [b_mg])
        def wout_mm(s, rows):
            bk = bank2()
            for half in range(2):
                for k in range(8):
                    P.pe(lambda e, s=s, rows=rows, half=half, k=k, bk=bk: e.matmul(
                        out=psum[0:rows, bk + half, :], lhsT=mg[:, k, s * 128:s * 128 + rows], rhs=w_out[:, k, half * 512:(half + 1) * 512],
                        start=(k == 0), stop=(k == 7)), [b_mg, b_wout], [pbank[bk + half]])
            return bk

        def resid_load(s, rows):
            xt, bl_xt = rslots[rcnt[0] % 2]
            rcnt[0] += 1
            P.dma("sp", xt[0:rows, :], x_rows_of(s), writes=bl_xt)
            return xt, bl_xt

        def resid_add(s, rows, bk, slot):
            xt, bl_xt = slot
            P.dve(lambda e, rows=rows, bk=bk, xt=xt: e.tensor_tensor(out=xt[0:rows, :].rearrange("p (h n) -> p h n", n=512),
                                                                      in0=xt[0:rows, :].rearrange("p (h n) -> p h n", n=512),
                                                                      in1=psum[0:rows, bk:bk + 2, :], op=ALU.add),
                  [pbank[bk], pbank[bk + 1]] + bl_xt, bl_xt)
            P.dma("sp", x1_rows_of(s), xt[0:rows, :], reads=bl_xt, writes=[x1b_of(s)])

        if len(subtiles) == 4:
            (s0, r0_), (s1, r1_), (s2, r2_), (s3, r3_) = subtiles
            l0 = resid_load(s0, r0_); l1 = resid_load(s1, r1_)
            if nx is not None:
                nx["b01"](); nx["f23"]()
            b0 = wout_mm(s0, r0_); b1 = wout_mm(s1, r1_)
            resid_add(s0, r0_, b0, l0); l2 = resid_load(s2, r2_)
            resid_add(s1, r1_, b1, l1); l3 = resid_load(s3, r3_)
            b2 = wout_mm(s2, r2_); b3 = wout_mm(s3, r3_)
            if nx is not None:
                nx["b23"]()
            resid_add(s2, r2_, b2, l2); resid_add(s3, r3_, b3, l3)
        else:
            for (s, rows) in subtiles:
                l = resid_load(s, rows)
                bk = wout_mm(s, rows)
                resid_add(s, rows, bk, l)

    P.dve(lambda e: e.memset(ua[:, :, 0:15], 0.0), [], [b_ua])
    for t in range(SEQ // TOK):
        r0 = t * TOK

        def xrows(tt, s):
            return D["x"][tt * TOK + s * 128:tt * TOK + (s + 1) * 128, :]

        if t == 0:
            front_only((0, 0), xrows(0, 0), 128)
            front_only((0, 1), xrows(0, 1), 128)
            back_only((0, 0), 128, 0)
            front_only((0, 2), xrows(0, 2), 128)
            back_only((0, 1), 128, 128)
            front_only((0, 3), xrows(0, 3), 128)
            back_only((0, 2), 128, 256)
            back_only((0, 3), 128, 384)
        win_uaub(TOK)
        for g in range(4):
            w = 2 << g
            src = ua[:, g, :]
            P.dve(lambda e, src=src: e.tensor_tensor(out=pa[:, 1:L], in0=src[:, 1:L], in1=src[:, 0:L - 1], op=ALU.add), [b_ua], [b_pa])
            cur, b_cur = pa, b_pa
            if g >= 1:
                P.dve(lambda e: e.tensor_tensor(out=pb[:, 3:L], in0=pa[:, 3:L], in1=pa[:, 1:L - 2], op=ALU.add), [b_pa], [b_pb])
                cur, b_cur = pb, b_pb
            if g >= 2:
                P.dve(lambda e: e.tensor_tensor(out=pa[:, 7:L], in0=pb[:, 7:L], in1=pb[:, 3:L - 4], op=ALU.add), [b_pb], [b_pa])
                cur, b_cur = pa, b_pa
            if g >= 3:
                P.dve(lambda e: e.tensor_tensor(out=pb[:, 15:L], in0=pa[:, 15:L], in1=pa[:, 7:L - 8], op=ALU.add), [b_pa], [b_pb])
                cur, b_cur = pb, b_pb
            if t == 0:
                P.dve(lambda e, cur=cur, g=g: e.tensor_tensor(out=cur[:, 15:31], in0=cur[:, 15:31], in1=c["poolcorr"][:, g, :], op=ALU.mult),
                      [b_cur, c["b_pc"]], [b_cur])
            P.dve(lambda e, cur=cur, g=g, w=w: e.scalar_tensor_tensor(out=pooled[:, g, :], in0=cur[:, 15:L], scalar=1.0 / w, in1=ua[:, g, 15:L],
                                                                      op0=ALU.mult, op1=ALU.subtract), [b_cur, b_ua], [b_pooled])
        if t == SEQ // TOK - 1:
            bk = bank()
            for k in range(8):
                P.pe(lambda e, k=k, bk=bk: e.matmul(out=psum[0:15, bk, :], lhsT=hT[:, k, TOK - 15:TOK], rhs=w_in[:, k, 0:512],
                                                    start=(k == 0), stop=(k == 7)), [b_hT, b_win], [pbank[bk]])
            P.act(lambda e, bk=bk: e.activation(out=osm[0:15, 0:512], in_=psum[0:15, bk, :], func=AF.Copy), [pbank[bk]], [b_osm])
            P.dma("sp", D["pool_p"], osm[0:15, 0:512], reads=[b_osm], writes=[outbuf], is_output=True)
        else:
            P.dve(lambda e: e.tensor_copy(out=pa[:, 0:60].rearrange("p (g n) -> p g n", n=15), in_=ua[:, :, TOK:TOK + 15]), [b_ua], [b_pa])
            P.dve(lambda e: e.tensor_copy(out=ua[:, :, 0:15], in_=pa[:, 0:60].rearrange("p (g n) -> p g n", n=15)), [b_pa], [b_ua])
        P.dve(lambda e: e.tensor_copy(out=SPb[:, :, :, 0:1], in_=tb[:, 5:7, :].unsqueeze(3)), [b_tb], [b_SPb])
        TB = [b_tb]
        for (o_, a1, b1, a2, b2, op) in ((tb[:, 7, :], tb[:, 2, :], tb[:, 5, :], tb[:, 3, :], tb[:, 6, :], ALU.subtract),
                                         (tb[:, 8, :], tb[:, 2, :], tb[:, 6, :], tb[:, 3, :], tb[:, 5, :], ALU.add)):
            P.dve(lambda e, a1=a1, b1=b1: e.tensor_tensor(out=tb[:, 9, :], in0=a1, in1=b1, op=ALU.mult), TB, TB)
            P.dve(lambda e, a2=a2, b2=b2: e.tensor_tensor(out=tb[:, 10, :], in0=a2, in1=b2, op=ALU.mult), TB, TB)
            P.dve(lambda e, o_=o_, op=op: e.tensor_tensor(out=o_, in0=tb[:, 9, :], in1=tb[:, 10, :], op=op), TB, TB)
        bkG = c["bank4"]()
        pGb = [[Buf("pG%d_%d" % (ql, cc)) for cc in range(4)] for ql in range(4)]
        for cc in range(4):
            for ri in range(2):
                for i in range(TCH):
                    for ql in range(4):
                        P.pe(lambda e, ql=ql, ri=ri, i=i, cc=cc, bkG=bkG: e.matmul(
                            out=psum[:, bkG + ql, cc * 128 + ri * 64:cc * 128 + (ri + 1) * 64], lhsT=W1[32 * ql:32 * ql + 32, cc, TCH - 1 - i, ri, :],
                            rhs=ub[32 * ql:32 * ql + 32, cc, i * NCH:(i + 1) * NCH], start=(i == 0), stop=(i == TCH - 1), tile_position=(32 * ql, 0)),
                             [b_W1, b_ub], [pbank[bkG + ql]])
        for cc in range(4):
            q0 = 4 * cc
            bk = bkG
            pG = psum[:, bk:bk + 4, cc * 128:(cc + 1) * 128].rearrange("p q (r n) -> p r q n", r=2)
            er = ER[:, q0:q0 + 4, :]; ei = EI[:, q0:q0 + 4, :]
            RB = [pbank[bk], pbank[bk + 1], pbank[bk + 2], pbank[bk + 3], b_ER, b_EI]
            erb = er.unsqueeze(1).broadcast_to([128, 2, 4, NCH])
            eib = ei.unsqueeze(1).broadcast_to([128, 2, 4, NCH])
            P.dve(lambda e, pG=pG, erb=erb: e.tensor_tensor(out=Tm, in0=pG, in1=erb, op=ALU.mult), RB, [b_Tm])
            P.dve(lambda e, pG=pG, eib=eib: e.tensor_tensor(out=St, in0=pG, in1=eib, op=ALU.mult), RB, [b_St])
            P.dve(lambda e: e.tensor_tensor(out=Gt[:, 0], in0=Tm[:, 0], in1=St[:, 1], op=ALU.add), [b_Tm, b_St], [b_Gt])
            P.dve(lambda e: e.tensor_tensor(out=Gt[:, 1], in0=Tm[:, 1], in1=St[:, 0], op=ALU.subtract), [b_Tm, b_St], [b_Gt])
            for ql in range(4):
                for ri in range(2):
                    P.dve(lambda e, ql=ql, ri=ri, q0=q0: e.tensor_tensor_scan(
                        out=St[:, ri, ql, :], data0=tb[:, 4, q0 + ql:q0 + ql + 1].broadcast_to([128, NCH]), data1=Gt[:, ri, ql, :],
                        initial=tb[:, 7 + ri, q0 + ql:q0 + ql + 1], op0=ALU.mult, op1=ALU.add), [b_tb, b_Gt], [b_St])
            P.dve(lambda e, erb=erb: e.tensor_tensor(out=Tm, in0=St, in1=erb, op=ALU.mult), [b_St, b_ER], [b_Tm])
            P.dve(lambda e, eib=eib: e.tensor_tensor(out=Gt, in0=St, in1=eib, op=ALU.mult), [b_St, b_EI], [b_Gt])
            P.dve(lambda e, q0=q0: e.tensor_tensor(out=SPb[:, 0, q0:q0 + 4, 1:65], in0=Tm[:, 0], in1=Gt[:, 1], op=ALU.subtract), [b_Tm, b_Gt], [b_SPb])
            P.dve(lambda e, q0=q0: e.tensor_tensor(out=tb[:, 5, q0:q0 + 4], in0=Tm[:, 0, :, 63], in1=Gt[:, 1, :, 63], op=ALU.subtract), [b_Tm, b_Gt], [b_tb])
            P.dve(lambda e, q0=q0: e.tensor_tensor(out=SPb[:, 1, q0:q0 + 4, 1:65], in0=Tm[:, 1], in1=Gt[:, 0], op=ALU.add), [b_Tm, b_Gt], [b_SPb])
            P.dve(lambda e, q0=q0: e.tensor_tensor(out=tb[:, 6, q0:q0 + 4], in0=Tm[:, 1, :, 63], in1=Gt[:, 0, :, 63], op=ALU.add), [b_Tm, b_Gt], [b_tb])
            gate_a_prepass(TOK, bkG, (2 * cc, 2 * cc + 1))
            if cc >= 2:
                gate_b_block(cc - 2, TOK, lambda bkG=bkG: bank_other(bkG))
            bk = bank_other(bkG)
            for j in range(TCH):
                for i in range(j + 1):
                    P.pe(lambda e, j=j, i=i, cc=cc, bk=bk: e.matmul(
                        out=psum[:, bk, j * NCH:(j + 1) * NCH], lhsT=KB_[:, cc, j - i, :], rhs=ub[:, cc, i * NCH:(i + 1) * NCH], start=(j == 0 and i == 0), stop=False,
                        skip_group_check=True), [b_KB, b_ub], [pbank[bk]])
            for j in range(TCH):
                for ri in range(2):
                    for ql in range(4):
                        P.pe(lambda e, j=j, ql=ql, ri=ri, q0=q0, bk=bk: e.matmul(
                            out=psum[32 * ql:32 * ql + 32, bk, j * NCH:(j + 1) * NCH], lhsT=W3[:, j, ri, q0 + ql, :], rhs=SPb[:, ri, q0 + ql, 0:NCH],
                            start=False, stop=(ri == 1), skip_group_check=True, tile_position=(0, 32 * ql)), [b_W3, b_SPb], [pbank[bk]])
            P.act(lambda e, cc=cc, bk=bk: e.activation(out=gl[:, cc, :].rearrange("p (c j) -> p j c", j=TCH),
                                                       in_=psum[:, bk, :].rearrange("p (j c) -> p j c", j=TCH), func=AF.Gelu_apprx_tanh), [pbank[bk]], [b_gl])
        if t == SEQ // TOK - 1:
            with c["nc"].allow_non_contiguous_dma(reason="final S5 state, 2048 elems"):
                P.dma("sp", D["re_p"].rearrange("(q g) p -> (g p) q", g=2), tb[:, 5, :], reads=[b_tb], writes=[outbuf], is_output=True, allow_slow_non_contiguous=True)
                P.dma("sp", D["im_p"].rearrange("(q g) p -> (g p) q", g=2), tb[:, 6, :], reads=[b_tb], writes=[outbuf], is_output=True, allow_slow_non_contiguous=True)
        if t + 1 < SEQ // TOK:
            def _f01(t=t):
                front_only((t + 1, 0), xrows(t + 1, 0), 128)
                front_only((t + 1, 1), xrows(t + 1, 1), 128)

            def _b01(t=t):
                back_only((t + 1, 0), 128, 0)
                back_only((t + 1, 1), 128, 128)

            def _f23(t=t):
                front_only((t + 1, 2), xrows(t + 1, 2), 128)
                front_only((t + 1, 3), xrows(t + 1, 3), 128)

            def _b23(t=t):
                back_only((t + 1, 2), 128, 256)
                back_only((t + 1, 3), 128, 384)
            pf = {"f01": _f01, "b01": _b01, "f23": _f23, "b23": _b23}
        else:
            pf = {"f01": (lambda: front_only("samp", D["xs"], NSAMP)), "b01": (lambda: None), "f23": (lambda: None), "b23": (lambda: None)}
        merge_and_out(TOK, [(s, 128) for s in range(4)],
                      lambda s, r0=r0: D["x"][r0 + s * 128:r0 + (s + 1) * 128, :],
                      lambda s, r0=r0: x1d[r0 + s * 128:r0 + (s + 1) * 128, :],
                      lambda s, r0=r0: x1buf[(r0 + s * 128) // 128], prefetch=pf, pre=True, gb_done=(0, 1))

    NS = NSAMP
    back_only("samp", NS, 0)
    win_uaub(NS)
    P.dma("sp", D["pool_s"][:, 0:14, :], D["spool"][:, 1:15, :], writes=[outbuf], is_output=True)
    bk = bank()
    for k in range(8):
        P.pe(lambda e, k=k, bk=bk: e.matmul(out=psum[0:NS, bk, :], lhsT=hT[:, k, 0:NS], rhs=w_in[:, k, 0:512],
                                            start=(k == 0), stop=(k == 7)), [b_hT, b_win], [pbank[bk]])
    P.act(lambda e, bk=bk: e.activation(out=osm[0:NS, 0:512], in_=psum[0:NS, bk, :], func=AF.Copy), [pbank[bk]], [b_osm])
    P.dma("sp", D["pool_s"][:, 14, :], osm[0:NS, 0:512], reads=[b_osm], writes=[outbuf], is_output=True)
    pbuf = Tm
    sp_tok = pb[0:120, 0:512]
    bufT = pa[:, 0:480].rearrange("p (h n) -> p h n", h=4)
    for hf in range(2):
        P.dma("sp", sp_tok, D["spool"][hf * 8:(hf + 1) * 8].rearrange("s r c -> (s r) c"), writes=[b_pb])
        bk = bank()
        for g in range(4):
            P.pe(lambda e, g=g, bk=bk: e.transpose(out=psum[:, bk, g * 120:(g + 1) * 120], in_=sp_tok[:, g * 128:(g + 1) * 128],
                                                   identity=identf[0:120, 0:120]), [b_pb, b_identf], [pbank[bk]])
        P.dve(lambda e, bk=bk: e.tensor_copy(out=pa[:, 0:480], in_=psum[:, bk, 0:480]), [pbank[bk]], [b_pa])
        bv = bufT.rearrange("p g (s r) -> p g s r", r=15)
        for g in range(4):
            w = 2 << g
            P.dve(lambda e, g=g, w=w, hf=hf, bv=bv: e.tensor_reduce(out=sa[:, g * 16 + hf * 8:g * 16 + hf * 8 + 8], in_=bv[:, g, :, 15 - (w - 1):15],
                                                                    axis=mybir.AxisListType.X, op=ALU.add), [b_pa], [b_sa])
    for g in range(4):
        w = 2 << g
        P.dve(lambda e, g=g: e.tensor_tensor(out=sa[:, g * 16:g * 16 + 16], in0=sa[:, g * 16:g * 16 + 16], in1=ua[:, g, 15:15 + NS], op=ALU.add),
              [b_sa, b_ua], [b_sa])
        P.dve(lambda e, g=g, w=w: e.scalar_tensor_tensor(out=pooled[:, g, 0:NS], in0=sa[:, g * 16:g * 16 + 16], scalar=1.0 / w, in1=ua[:, g, 15:15 + NS],
                                                         op0=ALU.mult, op1=ALU.subtract), [b_sa, b_ua], [b_pooled])
    H0, b_H0 = Gt, b_Gt
    H0v = Gt.rearrange("p r q n -> p r (q n)").rearrange("p r (q s) -> p r q s", s=NS)
    HNv = St.rearrange("p r q n -> p r (q n)").rearrange("p r (q s) -> p r q s", s=NS)
    T0v = Tm.rearrange("p r q n -> p r (q n)").rearrange("p r (q s) -> p r q s", s=NS)
    for ri, nm in enumerate(["sre", "sim"]):
        P.dma("sp", osm[0:NS, :], D[nm], writes=[b_osm])
        bk = bank()
        for q in range(16):
            P.pe(lambda e, q=q, bk=bk: e.transpose(out=psum[:, bk, q * NS:(q + 1) * NS], in_=osm[0:NS, q * 128:(q + 1) * 128],
                                                   identity=identf[0:NS, 0:NS]), [b_osm, b_identf], [pbank[bk]])
        P.dve(lambda e, ri=ri, bk=bk: e.tensor_copy(out=H0v[:, ri], in_=psum[:, bk, 0:256].rearrange("p (q s) -> p q s", s=NS)), [pbank[bk]], [b_Gt])
        P.dve(lambda e, ri=ri, bk=bk: e.tensor_copy(out=SPb[:, ri, :, 0:NS], in_=psum[:, bk, 0:256].rearrange("p (q s) -> p q s", s=NS)), [pbank[bk]], [b_SPb])
    bk = c["bank4"]()
    pBU = psum[:, bk:bk + 4, 0:128].rearrange("p l (r c s) -> p r c l s", r=2, c=4)
    for q in range(16):
        cc, ql = q // 4, q % 4
        for ri in range(2):
            P.pe(lambda e, q=q, cc=cc, ql=ql, ri=ri, bk=bk: e.matmul(out=psum[:, bk + ql, ri * 64 + cc * NS:ri * 64 + (cc + 1) * NS],
                                                                    lhsT=W1[32 * ql:32 * ql + 32, cc, 0, ri, :],
                                                                    rhs=ub[32 * ql:32 * ql + 32, cc, 0:NS], start=True, stop=True, tile_position=(32 * ql, 0)),
                 [b_W1, b_ub], [pbank[bk + ql]])
    a1r = tb[:, 0, :].unsqueeze(2).broadcast_to([128, 16, NS]); a1i = tb[:, 1, :].unsqueeze(2).broadcast_to([128, 16, NS])
    SS = [b_Gt, b_St, b_Tm, b_tb, pbank[bk], pbank[bk + 1], pbank[bk + 2], pbank[bk + 3]]
    P.dve(lambda e: e.tensor_tensor(out=T0v[:, 0], in0=H0v[:, 0], in1=a1r, op=ALU.mult), SS, SS)
    P.dve(lambda e: e.tensor_tensor(out=T0v[:, 1], in0=H0v[:, 1], in1=a1i, op=ALU.mult), SS, SS)
    P.dve(lambda e: e.tensor_tensor(out=T0v[:, 0], in0=T0v[:, 0], in1=T0v[:, 1], op=ALU.subtract), SS, SS)
    P.dve(lambda e: e.tensor_tensor(out=HNv[:, 0].rearrange("p (c l) s -> p c l s", l=4), in0=T0v[:, 0].rearrange("p (c l) s -> p c l s", l=4), in1=pBU[:, 0], op=ALU.add), SS, SS)
    P.dve(lambda e: e.tensor_tensor(out=T0v[:, 0], in0=H0v[:, 1], in1=a1r, op=ALU.mult), SS, SS)
    P.dve(lambda e: e.tensor_tensor(out=T0v[:, 1], in0=H0v[:, 0], in1=a1i, op=ALU.mult), SS, SS)
    P.dve(lambda e: e.tensor_tensor(out=T0v[:, 0], in0=T0v[:, 0], in1=T0v[:, 1], op=ALU.add), SS, SS)
    P.dve(lambda e: e.tensor_tensor(out=HNv[:, 1].rearrange("p (c l) s -> p c l s", l=4), in0=T0v[:, 0].rearrange("p (c l) s -> p c l s", l=4), in1=pBU[:, 1], op=ALU.add), SS, SS)
    for ri, nm in enumerate(["re_s", "im_s"]):
        bk4 = [bank() for _ in range(4)]
        for q in range(16):
            P.pe(lambda e, q=q, ri=ri, bk4=bk4: e.transpose(out=psum[0:NS, bk4[q // 4], (q % 4) * 128:(q % 4 + 1) * 128], in_=HNv[:, ri, q, :],
                                                            identity=identf), [b_St, b_identf], [pbank[bk4[q // 4]]])
        for b4 in range(4):
            P.act(lambda e, b4=b4, bk4=bk4: e.activation(out=osm[0:NS, b4 * 512:(b4 + 1) * 512], in_=psum[0:NS, bk4[b4], :], func=AF.Copy),
                  [pbank[bk4[b4]]], [b_osm])
        P.dma("sp", D[nm], osm[0:NS, :], reads=[b_osm], writes=[outbuf], is_output=True)
    for cc in range(4):
        q0 = 4 * cc
        bk = bank()
        P.pe(lambda e, cc=cc, bk=bk: e.matmul(out=psum[:, bk, 0:NS], lhsT=KB_[:, cc, 0, :], rhs=ub[:, cc, 0:NS], start=True, stop=False),
             [b_KB, b_ub], [pbank[bk]])
        for ql in range(4):
            for ri in range(2):
                P.pe(lambda e, ql=ql, ri=ri, q0=q0, bk=bk: e.matmul(out=psum[32 * ql:32 * ql + 32, bk, 0:NS], lhsT=W3[:, 0, ri, q0 + ql, :],
                                                                    rhs=SPb[:, ri, q0 + ql, 0:NS], start=False, stop=(ri == 1),
                                                                    tile_position=(0, 32 * ql)), [b_W3, b_SPb], [pbank[bk]])
        P.act(lambda e, cc=cc, bk=bk: e.activation(out=gl[:, cc, 0:NS], in_=psum[:, bk, 0:NS], func=AF.Gelu_apprx_tanh), [pbank[bk]], [b_gl])
    merge_and_out(NS, [(0, NS)], lambda s: D["xs"], lambda s: x1d[SEQ:SEQ + NS, :], lambda s: x1buf[SEQ // 128])


def phase_b(c):
    P = c["P"]; A = c["A"]; D = c["D"]; psum = c["psum"]; pbank = c["pbank"]; bank = c["bank"]; bank2 = c["bank2"]
    cols = c["cols"]; b_cols = c["b_cols"]; identf = c["identf"]; b_identf = c["b_identf"]
    x1d = c["x1d"]; x1buf = c["x1buf"]; outbuf = c["outbuf"]; held = c["held"]
    gfin, b_gfin = A("gfin", [128, DM], F32)
    P.dma("sp", gfin, D["g_final"].partition_broadcast(128), writes=[b_gfin])
    AR_ = c["AR_"]
    off1 = AR_.top
    wf1, b_wf1 = A("wf1", [128, 8, 4096], BF16)
    bw1 = [[AR_.sub_buf("wf1_%d_%d" % (k, h), off1, (k * 4096 + h * 2048) * 2, 4096) for h in range(2)] for k in range(8)]
    for h in range(2):
        for k in range(8):
            P.dma("pool", wf1[:, k, h * 2048:(h + 1) * 2048], D["w_ff1"][k * 128:(k + 1) * 128, h * 2048:(h + 1) * 2048], writes=[bw1[k][h]], max_dma_last_dim=4096)
    off2 = AR_.top
    wf2, b_wf2 = A("wf2", [128, 32, DM], BF16)
    bw2 = [AR_.sub_buf("wf2_%d" % k4, off2, k4 * 4 * DM * 2, 4 * DM * 2) for k4 in range(8)]
    for k4 in range(8):
        P.dma("pool", wf2[:, 4 * k4:4 * k4 + 4, :], D["w_ff2"][k4 * 512:(k4 + 1) * 512, :].rearrange("(k p) n -> p k n", p=128), writes=[bw2[k4]], max_dma_last_dim=4096)
    wpg, b_wpg = A("wpg", [128, 8, DM], BF16)
    for k2 in range(4):
        P.dma("pool", wpg[:, 2 * k2:2 * k2 + 2, :], D["w_ple_gate"][k2 * 256:(k2 + 1) * 256, :].rearrange("(k p) n -> p k n", p=128), writes=[b_wpg], group=True, max_dma_last_dim=4096)
    wpl, b_wpl = A("wpl", [128, 2, DM], BF16)
    P.dma("pool", wpl, D["w_ple"].rearrange("(k p) n -> p k n", p=128), writes=[b_wpl], max_dma_last_dim=4096)
    xts = [A("xtB%d" % i, [128, DM], F32) for i in range(4)]
    slots = []
    for i in range(2):
        xn_, b_xn_ = A("xnB%d" % i, [128, DM], BF16)
        ss_, b_ss_ = A("ssB%d" % i, [128, 4], F32)
        slots.append((xn_, b_xn_, ss_, b_ss_))
    ssf = [A("ssF%d" % i, [128, 4], F32) for i in range(2)]
    hTs = [A("hTB%d" % i, [128, 8, TOKB], BF16) for i in range(2)]
    offa = AR_.top
    aTf, b_aT = A("aT", [128, 16 * TOKB], F32)
    aT = aTf.bitcast(BF16).rearrange("p (k n) -> p k n", n=TOKB)
    b_aTlo = AR_.sub_buf("aT_lo", offa, 0, 24 * TOKB * 2)
    b_aThi = AR_.sub_buf("aT_hi", offa, 24 * TOKB * 2, 8 * TOKB * 2)
    sig_hi = aTf[:, 12 * TOKB:16 * TOKB]

    def b_aT_of(fc):
        return b_aTlo if fc < 24 else b_aThi
    rts = [A("rt%d" % i, [128, TOKB], F32) for i in range(2)]
    pts = [A("pt%d" % i, [128, 256], F32) for i in range(2)]
    pTs = [A("pT%d" % i, [128, 2, 128], BF16) for i in range(2)]
    sigs = [A("sig0", [128, DM], F32), (sig_hi, b_aThi)]

    NT = SEQ // TOKB
    tiles = []
    for t in range(NT):
        r0 = t * TOKB
        tiles.append(dict(subs=[(0, 128), (1, 128)], ntok=TOKB,
                          x1rows=(lambda s, r0=r0: x1d[r0 + s * 128:r0 + (s + 1) * 128, :]),
                          x1b=(lambda s, r0=r0: x1buf[(r0 + s * 128) // 128]),
                          prows=(lambda s, r0=r0: D["p"][r0 + s * 128:r0 + (s + 1) * 128, :]),
                          yrows=(lambda s, r0=r0: D["y"][r0 + s * 128:r0 + (s + 1) * 128, :])))
    tiles.append(dict(subs=[(0, NSAMP)], ntok=NSAMP, x1rows=(lambda s: x1d[SEQ:SEQ + NSAMP, :]), x1b=(lambda s: x1buf[SEQ // 128]),
                      prows=(lambda s: D["ps"]), yrows=(lambda s: D["ys"])))

    def xt_of(t, s):
        return xts[2 * (t % 2) + s]

    def front(t):
        T = tiles[t]
        for (s, rows) in T["subs"]:
            xt, b_xt = xt_of(t, s)
            P.dma("sp", xt[0:rows, :], T["x1rows"](s), reads=[T["x1b"](s)], writes=[b_xt])
            _norm_front(c, xt[0:rows, :], b_xt, rows, slots[s])

    def back(t):
        T = tiles[t]
        hT, b_hT = hTs[t % 2]
        for (s, rows) in T["subs"]:
            _norm_back(c, rows, 1, hT, b_hT, s * 128, slots[s])

    def ffn2(t, s, rows, mid=None):
        xt, b_xt = xt_of(t, s)
        bk = bank2()
        for half in range(2):
            if half == 1 and mid is not None:
                mid()
            for fc in range(32):
                P.pe(lambda e, s=s, rows=rows, half=half, fc=fc, bk=bk: e.matmul(
                    out=psum[0:rows, bk + half, :], lhsT=aT[:, fc, s * 128:s * 128 + rows], rhs=wf2[:, fc, half * 512:(half + 1) * 512],
                    start=(fc == 0), stop=(fc == 31)), [b_aT_of(fc), bw2[fc // 4]], [pbank[bk + half]])
        P.dve(lambda e, rows=rows, bk=bk, xt=xt: e.tensor_tensor(out=xt[0:rows, :].rearrange("p (h n) -> p h n", n=512),
                                                                  in0=xt[0:rows, :].rearrange("p (h n) -> p h n", n=512),
                                                                  in1=psum[0:rows, bk:bk + 2, :], op=ALU.add),
              [pbank[bk], pbank[bk + 1], b_xt], [b_xt])

    pending = []

    def final_norm(t, subs):
        T = tiles[t]
        for (s, rows) in subs:
            xt, b_xt = xt_of(t, s)
            ss, b_ss = ssf[s]
            junk, b_junk = slots[s][0], slots[s][1]
            P.act(lambda e, rows=rows, xt=xt, junk=junk, ss=ss: e.activation(out=junk[0:rows, :], in_=xt[0:rows, :], func=AF.Square, accum_out=ss[0:rows, 0:1]),
                  [b_xt], [b_junk, b_ss])
            P.dve(lambda e, rows=rows, ss=ss: e.tensor_scalar(out=ss[0:rows, 1:2], in0=ss[0:rows, 0:1], scalar1=1.0 / DM, scalar2=1e-6, op0=ALU.mult, op1=ALU.add), [b_ss], [b_ss])
            P.act(lambda e, rows=rows, ss=ss: e.activation(out=ss[0:rows, 1:2], in_=ss[0:rows, 1:2], func=AF.Sqrt), [b_ss], [b_ss])
            P.dve(lambda e, rows=rows, ss=ss: e.reciprocal(out=ss[0:rows, 2:3], in_=ss[0:rows, 1:2]), [b_ss], [b_ss])
            P.dve(lambda e, rows=rows, xt=xt, ss=ss: e.scalar_tensor_tensor(out=xt[0:rows, :], in0=xt[0:rows, :], scalar=ss[0:rows, 2:3], in1=gfin[0:rows, :],
                                                                            op0=ALU.mult, op1=ALU.mult), [b_xt, b_ss, b_gfin], [b_xt])
            P.dma("sp", T["yrows"](s), xt[0:rows, :], reads=[b_xt], writes=[outbuf], is_output=True)

    front(0)
    back(0)
    for t in range(len(tiles)):
        T = tiles[t]
        ntok = T["ntok"]
        hT, b_hT = hTs[t % 2]
        for (s, rows) in T["subs"]:
            pt, b_pt = pts[s]
            P.dma("sp", pt[0:rows, :], T["prows"](s), writes=[b_pt])
        if ntok <= 128:
            for f4 in range(8):
                bk = bank()
                for j in range(4):
                    fc = 4 * f4 + j
                    for k in range(8):
                        P.pe(lambda e, fc=fc, j=j, k=k, bk=bk, hT=hT, ntok=ntok: e.matmul(out=psum[:, bk, j * ntok:(j + 1) * ntok], lhsT=wf1[:, k, fc * 128:(fc + 1) * 128],
                                                                                      rhs=hT[:, k, 0:ntok], start=(k == 0), stop=(k == 7)), [bw1[k][fc // 16], b_hT], [pbank[bk]])
                rt, b_rt = rts[f4 % 2]
                P.act(lambda e, bk=bk, rt=rt, ntok=ntok: e.activation(out=rt[:, 0:4 * ntok], in_=psum[:, bk, 0:4 * ntok], func=AF.Relu), [pbank[bk]], [b_rt])
                P.dve(lambda e, f4=f4, bk=bk, rt=rt, ntok=ntok: e.tensor_tensor(out=aT[:, 4 * f4:4 * f4 + 4, 0:ntok],
                                                                                  in0=psum[:, bk, 0:4 * ntok].rearrange("p (j n) -> p j n", n=ntok),
                                                                                  in1=rt[:, 0:4 * ntok].rearrange("p (j n) -> p j n", n=ntok), op=ALU.mult),
                      [pbank[bk], b_rt], [b_aT_of(4 * f4)])
                if f4 == 1:
                    while pending:
                        final_norm(*pending.pop(0))
        else:
            for fc in range(32):
                bk = bank()
                for k in range(8):
                    P.pe(lambda e, fc=fc, k=k, bk=bk, hT=hT, ntok=ntok: e.matmul(out=psum[:, bk, 0:ntok], lhsT=wf1[:, k, fc * 128:(fc + 1) * 128], rhs=hT[:, k, 0:ntok],
                                                                      start=(k == 0), stop=(k == 7)), [bw1[k][fc // 16], b_hT], [pbank[bk]])
                rt, b_rt = rts[fc % 2]
                P.act(lambda e, bk=bk, rt=rt, ntok=ntok: e.activation(out=rt[:, 0:ntok], in_=psum[:, bk, 0:ntok], func=AF.Relu), [pbank[bk]], [b_rt])
                P.dve(lambda e, fc=fc, bk=bk, rt=rt, ntok=ntok: e.tensor_tensor(out=aT[:, fc, 0:ntok], in0=psum[:, bk, 0:ntok], in1=rt[:, 0:ntok], op=ALU.mult),
                      [pbank[bk], b_rt], [b_aT_of(fc)])
                if fc == 7:
                    while pending:
                        final_norm(*pending.pop(0))
        if t + 1 < len(tiles):
            front(t + 1)
        subs = T["subs"]
        ffn2(t, *subs[0])
        if t + 1 < len(tiles):
            back(t + 1)
        xt0, b_xt0 = xt_of(t, 0)
        _norm_front(c, xt0[0:subs[0][1], :], b_xt0, subs[0][1], slots[0])
        if len(subs) > 1:
            ffn2(t, *subs[1], mid=(lambda hT=hT, b_hT=b_hT, r0_=subs[0][1]: _norm_back(c, r0_, 2, hT, b_hT, 0, slots[0])))
            xt1, b_xt1 = xt_of(t, 1)
            _norm_front(c, xt1[0:subs[1][1], :], b_xt1, subs[1][1], slots[1])
        gate_bk = {}
        wpl_bk = {}

        def ple_pT(s, rows):
            pt, b_pt = pts[s]; pT, b_pT = pTs[s]
            bk = bank()
            for k in range(2):
                P.pe(lambda e, k=k, rows=rows, bk=bk, pt=pt: e.transpose(out=psum[:, bk, k * 128:k * 128 + rows], in_=pt[0:rows, k * 128:(k + 1) * 128],
                                                                        identity=identf[0:rows, 0:rows]), [b_pt, b_identf], [pbank[bk]])
            P.act(lambda e, rows=rows, bk=bk, pT=pT: e.activation(out=pT[:, :, 0:rows], in_=psum[:, bk, 0:256].rearrange("p (k n) -> p k n", n=128)[:, :, 0:rows],
                                                                  func=AF.Copy), [pbank[bk]], [b_pT])

        def ple_gate(s, rows):
            bk = bank2()
            gate_bk[s] = bk
            for half in range(2):
                for k in range(8):
                    P.pe(lambda e, s=s, rows=rows, half=half, k=k, bk=bk, hT=hT: e.matmul(
                        out=psum[0:rows, bk + half, :], lhsT=hT[:, k, s * 128:s * 128 + rows], rhs=wpg[:, k, half * 512:(half + 1) * 512],
                        start=(k == 0), stop=(k == 7)), [b_hT, b_wpg], [pbank[bk + half]])
            sig, b_sig = sigs[s]
            P.act(lambda e, rows=rows, bk=bk, sig=sig: e.activation(out=sig[0:rows, :].rearrange("p (h n) -> p h n", n=512), in_=psum[0:rows, bk:bk + 2, :],
                                                                    func=AF.Sigmoid), [pbank[bk], pbank[bk + 1]], [b_sig])

        def ple_wpl(s, rows):
            pT, b_pT = pTs[s]
            bk = bank2()
            wpl_bk[s] = bk
            held.update((bk, bk + 1))
            for half in range(2):
                for k in range(2):
                    P.pe(lambda e, rows=rows, half=half, k=k, bk=bk, pT=pT: e.matmul(
                        out=psum[0:rows, bk + half, :], lhsT=pT[:, k, 0:rows], rhs=wpl[:, k, half * 512:(half + 1) * 512],
                        start=(k == 0), stop=(k == 1)), [b_pT, b_wpl], [pbank[bk + half]])

        if len(subs) == 2:
            (s0, r0_), (s1, r1_) = subs
            ple_pT(s0, r0_); ple_pT(s1, r1_)
            ple_gate(s0, r0_)
            ple_wpl(s0, r0_); ple_wpl(s1, r1_)
            _norm_back(c, r1_, 2, hT, b_hT, s1 * 128, slots[s1])
            ple_gate(s1, r1_)
        else:
            for (s, rows) in subs:
                _norm_back(c, rows, 2, hT, b_hT, s * 128, slots[s])
                ple_pT(s, rows)
                ple_gate(s, rows)
                ple_wpl(s, rows)
        for (s, rows) in subs:
            xt, b_xt = xt_of(t, s)
            sig, b_sig = sigs[s]
            bk = wpl_bk[s]
            held.difference_update((bk, bk + 1))
            P.dve(lambda e, rows=rows, bk=bk, sig=sig: e.tensor_tensor(out=sig[0:rows, :].rearrange("p (h n) -> p h n", n=512),
                                                                        in0=sig[0:rows, :].rearrange("p (h n) -> p h n", n=512),
                                                                        in1=psum[0:rows, bk:bk + 2, :], op=ALU.mult),
                  [pbank[bk], pbank[bk + 1], b_sig], [b_sig])
            P.dve(lambda e, rows=rows, xt=xt, sig=sig: e.tensor_tensor(out=xt[0:rows, :], in0=xt[0:rows, :], in1=sig[0:rows, :], op=ALU.add), [b_xt, b_sig], [b_xt])
        pending.append((t, subs))
    while pending:
        final_norm(*pending.pop(0))


_NC_CACHE = {}


def _consts():
    identf = np.eye(128, dtype=np.float32)
    pc = np.ones((128, 4, 16), dtype=np.float32)
    for g in range(4):
        w = 2 << g
        for t in range(16):
            pc[:, g, t] = float(w) / float(min(t + 1, w))
    return identf, pc


def kernel(x_prompt, x_sample, p_prompt, p_sample, state_pool, state_ssm_re, state_ssm_im,
           g_mix, w_in, w_pool, pool_scale, lam_re, lam_im, log_dt, b_re, b_im, c_re, c_im,
           d_skip, w_glu_v, w_glu_g, w_out, g_ff, w_ff1, w_ff2, g_ple, w_ple, w_ple_gate, g_final):
    f = lambda a: np.ascontiguousarray(np.asarray(a, dtype=np.float32))
    if "nc" not in _NC_CACHE:
        _NC_CACHE["nc"] = build_nc()
    nc = _NC_CACHE["nc"]
    identf, pc = _consts()
    shared = {
        "g_mix": f(g_mix)[0], "w_in": f(w_in)[0], "w_pool": f(w_pool)[0], "pool_scale": f(pool_scale)[0],
        "lam_re": f(lam_re)[0], "lam_im": f(lam_im)[0], "log_dt": f(log_dt)[0], "b_re": f(b_re)[0], "b_im": f(b_im)[0],
        "c_re": f(c_re)[0], "c_im": f(c_im)[0], "d_skip": f(d_skip)[0], "w_glu_v": f(w_glu_v)[0], "w_glu_g": f(w_glu_g)[0],
        "w_out": f(w_out)[0], "g_ff": f(g_ff)[0], "w_ff1": f(w_ff1)[0], "w_ff2": f(w_ff2)[0], "g_ple": f(g_ple)[0],
        "w_ple": f(w_ple)[0], "w_ple_gate": f(w_ple_gate)[0], "g_final": f(g_final), "identf": identf, "poolcorr": pc,
    }
    xp = f(x_prompt); xs = f(x_sample); pp = f(p_prompt); ps = f(p_sample)
    sp = f(state_pool); sr = f(state_ssm_re); si = f(state_ssm_im)
    in_maps = []
    for i in range(8):
        m = dict(shared)
        sl = slice(i * NSAMP, (i + 1) * NSAMP)
        m.update({"x": xp[i], "xs": np.ascontiguousarray(xs[sl, 0]), "p": pp[0, i], "ps": np.ascontiguousarray(ps[0, sl, 0]),
                  "spool": np.ascontiguousarray(sp[0, sl]), "sre": np.ascontiguousarray(sr[0, sl].reshape(NSAMP, 2048)),
                  "sim": np.ascontiguousarray(si[0, sl].reshape(NSAMP, 2048))})
        in_maps.append(m)
    res = run_bass_kernel_spmd(nc, in_maps, core_ids=list(range(8)))
    R = res.results
    y_prompt = np.stack([R[i]["y"] for i in range(8)], 0)
    y_sample = np.concatenate([R[i]["ys"] for i in range(8)], 0)[:, None, :]
    pool_p = np.stack([R[i]["pool_p"] for i in range(8)], 0)[None]
    pool_s = np.concatenate([R[i]["pool_s"] for i in range(8)], 0)[None]
    re_p = np.stack([R[i]["re_p"] for i in range(8)], 0)[None]
    im_p = np.stack([R[i]["im_p"] for i in range(8)], 0)[None]
    re_s = np.concatenate([R[i]["re_s"] for i in range(8)], 0).reshape(1, 128, 32, 64)
    im_s = np.concatenate([R[i]["im_s"] for i in range(8)], 0).reshape(1, 128, 32, 64)
    return (y_prompt, y_sample, pool_p, pool_s, re_p, im_p, re_s, im_s)
```

```python
import math
import numpy as np
import ml_dtypes
import concourse.bass as bass
import concourse.mybir as mybir
from concourse.bass_utils import run_bass_kernel_spmd

F32 = mybir.dt.float32
BF16 = mybir.dt.bfloat16
ALU = mybir.AluOpType
AF = mybir.ActivationFunctionType


class Buf:
    __slots__ = ("name", "last_w", "readers", "grp_deps", "grp_open")

    def __init__(self, name):
        self.name = name
        self.last_w = []
        self.readers = []
        self.grp_deps = []
        self.grp_open = False


class Op:
    __slots__ = ("eng", "fn", "deps", "is_dma", "sig", "sem", "val", "idx", "prev_dma")

    def __init__(self, eng, fn, deps, is_dma):
        self.eng = eng
        self.fn = fn
        self.deps = deps
        self.is_dma = is_dma
        self.sig = False
        self.sem = None
        self.val = 0
        self.idx = -1
        self.prev_dma = None


ENGS = ("pe", "act", "dve", "pool", "sp")
N_HW_SEMS = 32
N_SW_SEMS = 24
N_DMA_SEMS = N_HW_SEMS + N_SW_SEMS


class Prog:
    def __init__(self, nc):
        self.nc = nc
        self.ops = {e: [] for e in ENGS}
        self.all_ops = []
        self.n_dma = 0
        self.n_hw = 0
        self.n_sw = 0
        self.dma_last = [None] * N_DMA_SEMS
        self.dma_cnt = [0] * N_DMA_SEMS
        self.out_dmas = []

    def op(self, eng, fn, reads=(), writes=(), dma=False, is_output=False, group=False):
        deps = []
        for b in reads:
            deps.extend(b.last_w)
        for b in writes:
            if group and b.grp_open:
                deps.extend(b.grp_deps)
            else:
                deps.extend(b.last_w)
                deps.extend(b.readers)
        o = Op(eng, fn, deps, dma)
        if dma:
            if eng == "pool":
                slot = N_HW_SEMS + self.n_sw % N_SW_SEMS
                self.n_sw += 1
            else:
                slot = self.n_hw % N_HW_SEMS
                self.n_hw += 1
            self.n_dma += 1
            o.prev_dma = self.dma_last[slot]
            self.dma_last[slot] = o
            self.dma_cnt[slot] += 1
            o.sem = slot
            o.val = 16 * self.dma_cnt[slot]
            o.sig = True
            if is_output:
                self.out_dmas.append(o)
        for b in reads:
            b.readers.append(o)
            b.grp_open = False
        for b in writes:
            if group and b.grp_open:
                b.last_w.append(o)
            else:
                b.grp_deps = list(b.last_w) + list(b.readers)
                b.last_w = [o]
                b.readers = []
                b.grp_open = group
        o.idx = len(self.all_ops)
        self.all_ops.append(o)
        self.ops[eng].append(o)
        return o

    def pe(self, fn, reads=(), writes=()):
        return self.op("pe", fn, reads, writes)

    def act(self, fn, reads=(), writes=()):
        return self.op("act", fn, reads, writes)

    def dve(self, fn, reads=(), writes=()):
        return self.op("dve", fn, reads, writes)

    def pool(self, fn, reads=(), writes=()):
        return self.op("pool", fn, reads, writes)

    def dma(self, eng, out, in_, reads=(), writes=(), is_output=False, group=False, **kw):
        return self.op(eng, lambda e: e.dma_start(out=out, in_=in_, **kw), reads, writes,
                       dma=True, is_output=is_output, group=group)

    def emit(self, block_ctx_sems):
        nc = self.nc
        eng_sem = block_ctx_sems["eng"]
        dma_sem = block_ctx_sems["dma"]
        for o in self.all_ops:
            for d in o.deps:
                if not d.is_dma:
                    if d.eng == "pe" and o.eng == "pe" and not o.is_dma:
                        continue
                    d.sig = True
        for e in ENGS:
            c = 0
            for o in self.ops[e]:
                if o.is_dma:
                    continue
                if o.sig:
                    c += 1
                    o.val = c
                    o.sem = e
        final_waits = {}
        for o in self.out_dmas:
            final_waits[o.sem] = max(final_waits.get(o.sem, 0), o.val)

        def run_engine(ename, eng):
            seen = {}
            for o in self.ops[ename]:
                need = {}
                for d in o.deps:
                    if (not d.is_dma) and d.eng == "pe" and ename == "pe" and not o.is_dma:
                        continue
                    key = ("d", d.sem) if d.is_dma else ("e", d.sem)
                    if d.val > need.get(key, 0):
                        need[key] = d.val
                if o.is_dma and o.prev_dma is not None:
                    key = ("d", o.prev_dma.sem)
                    if o.prev_dma.val > need.get(key, 0):
                        need[key] = o.prev_dma.val
                for key, v in need.items():
                    if seen.get(key, 0) >= v:
                        continue
                    seen[key] = v
                    s = dma_sem[key[1]] if key[0] == "d" else eng_sem[key[1]]
                    eng.wait_ge(s, v)
                ins = o.fn(eng)
                if o.is_dma:
                    ins.then_inc(dma_sem[o.sem], 16)
                elif o.sig:
                    ins.then_inc(eng_sem[o.sem], 1)
            if ename == "sp":
                for s, v in final_waits.items():
                    if seen.get(("d", s), 0) < v:
                        eng.wait_ge(dma_sem[s], v)

        return run_engine


class Arena:
    def __init__(self, handle, nbytes):
        self.h = handle
        self.nbytes = nbytes
        self.top = 0
        self.live = []
        self.freed = []

    def alloc(self, name, shape, dt):
        esz = 4 if dt == F32 else 2
        n = 1
        for s in shape[1:]:
            n *= s
        size = (n * esz + 3) // 4 * 4
        off = self.top
        self.top += size
        assert self.top <= self.nbytes, ("SBUF arena overflow", name, self.top)
        v = self.h[0:shape[0], off // 4:(off + size) // 4]
        if dt != F32:
            v = v.bitcast(dt)[:, 0:n]
        if len(shape) == 3:
            v = v.rearrange("p (a b) -> p a b", b=shape[2])
        elif len(shape) == 4:
            v = v.rearrange("p (a b c) -> p a b c", b=shape[2], c=shape[3])
        elif len(shape) == 5:
            v = v.rearrange("p (a b c d) -> p a b c d", b=shape[2], c=shape[3], d=shape[4])
        buf = Buf(name)
        keep = []
        for (o, s, b) in self.freed:
            if o < off + size and off < o + s:
                buf.readers.extend(b.last_w)
                buf.readers.extend(b.readers)
            keep.append((o, s, b))
        self.freed = keep
        self.live.append((off, size, buf))
        return v, buf

    def sub_buf(self, name, base_off, rel_off, size):
        off = base_off + rel_off
        buf = Buf(name)
        for (o, s_, b) in self.freed:
            if o < off + size and off < o + s_:
                buf.readers.extend(b.last_w)
                buf.readers.extend(b.readers)
        return buf

    def mark(self):
        return self.top

    def release(self, mark):
        nl = []
        for (o, s, b) in self.live:
            if o >= mark:
                self.freed.append((o, s, b))
            else:
                nl.append((o, s, b))
        self.live = nl
        self.top = mark


def dram_ap(ap, offset, pattern):
    return bass.AP(ap.tensor, offset, pattern)


SEQ = 2048
DM = 1024
NSAMP = 16
TOK = 512
TOKB = 256
TCH = 8
ARENA_BYTES = 212000
GC1 = 2.0 * math.sqrt(2.0 / math.pi)


def build_nc():
    from contextlib import ExitStack
    nc = bass.Bass("TRN2", target_bir_lowering=False)
    D = {}

    def din(name, shape, dt=F32):
        D[name] = nc.dram_tensor(name, list(shape), dt, kind="ExternalInput").ap()

    def dout(name, shape):
        D[name] = nc.dram_tensor(name, list(shape), F32, kind="ExternalOutput").ap()

    din("x", [SEQ, DM]); din("xs", [NSAMP, DM]); din("p", [SEQ, 256]); din("ps", [NSAMP, 256])
    din("spool", [NSAMP, 15, 512]); din("sre", [NSAMP, 2048]); din("sim", [NSAMP, 2048])
    din("g_mix", [DM]); din("w_in", [DM, 3072]); din("w_pool", [4, 128, 256]); din("pool_scale", [DM])
    din("lam_re", [32, 64]); din("lam_im", [32, 64]); din("log_dt", [32])
    din("b_re", [32, 64, 16]); din("b_im", [32, 64, 16]); din("c_re", [32, 16, 64]); din("c_im", [32, 16, 64])
    din("d_skip", [512]); din("w_glu_v", [512, DM]); din("w_glu_g", [512, DM]); din("w_out", [DM, DM])
    din("g_ff", [DM]); din("w_ff1", [DM, 4096]); din("w_ff2", [4096, DM]); din("g_ple", [DM])
    din("w_ple", [256, DM]); din("w_ple_gate", [DM, DM]); din("g_final", [DM])
    din("identf", [128, 128]); din("poolcorr", [128, 4, 16])
    dout("y", [SEQ, DM]); dout("ys", [NSAMP, DM]); dout("pool_p", [15, 512]); dout("pool_s", [NSAMP, 15, 512])
    dout("re_p", [32, 64]); dout("im_p", [32, 64]); dout("re_s", [NSAMP, 2048]); dout("im_s", [NSAMP, 2048])
    x1d = nc.dram_tensor("x1_scratch", [SEQ + NSAMP, DM], F32).ap()

    P = Prog(nc)
    es = ExitStack()
    with es:
        arena_h = es.enter_context(nc.sbuf_tensor("arena", [128, ARENA_BYTES // 4], F32))
        psum = es.enter_context(nc.psum_tensor("psum", [128, 8, 512], F32))
        AR_ = Arena(arena_h, ARENA_BYTES)
        A = AR_.alloc
        pbank = [Buf("bank%d" % i) for i in range(8)]
        bank_ctr = [0]

        held = set()

        def bank():
            b = bank_ctr[0] % 8
            bank_ctr[0] += 1
            assert b not in held, ("PSUM bank handed out while a deferred reader is pending", b)
            return b

        def bank2():
            if bank_ctr[0] % 2:
                bank_ctr[0] += 1
            b = bank_ctr[0] % 8
            bank_ctr[0] += 2
            assert b not in held and (b + 1) not in held, ("PSUM bank handed out while a deferred reader is pending", b)
            return b

        def bank4():
            while bank_ctr[0] % 4:
                bank_ctr[0] += 1
            b = bank_ctr[0] % 8
            bank_ctr[0] += 4
            return b

        x1buf = [Buf("x1d%d" % i) for i in range(SEQ // 128 + 1)]
        outbuf = Buf("outs")

        identf, b_identf = A("identf", [128, 128], F32)
        P.dma("sp", identf, D["identf"], writes=[b_identf])
        identb, b_identb = A("identb", [128, 128], BF16)
        cols, b_cols = A("cols", [128, 5, 8], F32)
        stg, b_stg = A("stg", [64, 128], F32)

        def emit_cols():
            P.dve(lambda e: e.tensor_copy(out=identb, in_=identf), [b_identf], [b_identb])
            for i, nm in enumerate(["g_mix", "g_ff", "g_ple", "pool_scale"]):
                P.dma("sp", stg[8 * i:8 * i + 8, :], D[nm].rearrange("(c p) -> c p", p=128), writes=[b_stg], group=True)
            P.dma("sp", stg[32:36, :], D["d_skip"].rearrange("(c p) -> c p", p=128), writes=[b_stg], group=True)
            bk0 = bank()
            P.pe(lambda e: e.transpose(out=psum[:, bk0, 0:36], in_=stg[0:36, :], identity=identf[0:36, 0:36]), [b_stg, b_identf], [pbank[bk0]])
            P.dve(lambda e: e.tensor_copy(out=cols.rearrange("p a b -> p (a b)")[:, 0:36], in_=psum[:, bk0, 0:36]), [pbank[bk0]], [b_cols])
        mark0 = AR_.mark()

        W1, b_W1 = A("W1", [128, 4, 8, 2, 128], BF16)
        W3f, b_W3 = A("W3f", [128, 9, 2, 16, 32], BF16)
        W3 = W3f[:, 1:9]
        KB_, b_KB = A("Kblk", [128, 4, 8, 128], BF16)
        ER, b_ER = A("ER", [128, 16, 64], F32)
        EI, b_EI = A("EI", [128, 16, 64], F32)
        tb, b_tb = A("tb", [128, 12, 16], F32)
        poolcorr, b_pc = A("poolcorr", [128, 4, 16], F32)
        P.dma("sp", poolcorr, D["poolcorr"], writes=[b_pc])
        w_in, b_win = A("w_in", [128, 8, 3072], BF16)
        for k in range(8):
            P.dma("pool", w_in[:, k, :], D["w_in"][k * 128:(k + 1) * 128, :], writes=[b_win], group=True, max_dma_last_dim=4096)
        wgv, b_wgv = A("wgv", [128, 4, DM], BF16)
        wgg, b_wgg = A("wgg", [128, 4, DM], BF16)
        for k in range(4):
            P.dma("pool", wgv[:, k, :], D["w_glu_v"][k * 128:(k + 1) * 128, :], writes=[b_wgv], group=True, max_dma_last_dim=4096)
            P.dma("pool", wgg[:, k, :], D["w_glu_g"][k * 128:(k + 1) * 128, :], writes=[b_wgg], group=True, max_dma_last_dim=4096)
        wpool, b_wpool = A("wpool", [128, 4, 256], BF16)
        P.dma("pool", wpool, D["w_pool"].rearrange("g p n -> p g n"), writes=[b_wpool])
        markA = AR_.mark()

        prm, b_prm = A("prm", [128, 24, 16], F32)
        stgL, b_stgL = A("stgL", [32, 2, 2, 64], F32)
        ldtb, b_ldtb = A("ldtb", [128, 32], F32)
        PW, b_PW = A("PW", [128, 2, 9, 16], F32)
        Bb, b_Bb = A("Bb", [128, 2, 16, 32], F32)
        Cb, b_Cb = A("Cb", [128, 2, 16, 32], F32)
        Cn, b_Cn = A("Cn", [128, 2, 4, 2, 64], F32)
        KBk, b_KBk = A("KBk", [128, 2, 16, 32], F32)
        T1, b_T1 = A("T1s", [128, 4, 16, 32], F32)
        T2, b_T2 = A("T2s", [128, 4, 16, 32], F32)
        U1, b_U1 = A("U1s", [128, 2, 16, 32], F32)
        U2, b_U2 = A("U2s", [128, 2, 16, 32], F32)
        ABa, b_ABa = A("ABa", [128, 2, 8, 16, 32], BF16)
        Bpad, b_Bpad = A("Bpad", [128, 2, 16, 128], BF16)

        def V(fn, r, w):
            return P.dve(fn, r, w)

        def prmv(i):
            return prm[:, i, :]

        S_ = [b_prm]
        V(lambda e: e.memset(Bb, 0.0), [], [b_Bb])
        V(lambda e: e.memset(Cb, 0.0), [], [b_Cb])
        V(lambda e: e.memset(Bpad, 0.0), [], [b_Bpad])
        for ti, nm in enumerate(["lam_re", "lam_im"]):
            for dup in range(2):
                P.dma("sp", stgL[:, ti, dup, :], D[nm], writes=[b_stgL], group=True)
        P.dma("sp", ldtb, D["log_dt"].partition_broadcast(128), writes=[b_ldtb])
        for ri, nm in enumerate(["b_re", "b_im"]):
            for gl in range(2):
                P.dma("act", Bb[gl * 64:(gl + 1) * 64, ri, :, gl * 16:(gl + 1) * 16],
                      dram_ap(D[nm], gl * 1024, [[16, 64], [2048, 16], [1, 16]]), writes=[b_Bb], group=True)
        for ri, nm in enumerate(["c_re", "c_im"]):
            for dup in range(2):
                P.dma("sp", Cn[:, ri, :, dup, :],
                      D[nm].rearrange("g h p -> (g h) p").rearrange("(c r) p -> r c p", r=128), writes=[b_Cn], group=True)
        for ti in range(2):
            bkL = bank()
            P.pe(lambda e, ti=ti, bkL=bkL: e.transpose(out=psum[:, bkL, 0:32], in_=stgL[:, ti, :, :].rearrange("p a b -> p (a b)"),
                                                       identity=identf[0:32, 0:32]), [b_stgL, b_identf], [pbank[bkL]])
            for gl in range(2):
                P.dve(lambda e, ti=ti, gl=gl, bkL=bkL: e.tensor_copy(out=prm[gl * 64:(gl + 1) * 64, ti, :], in_=psum[gl * 64:(gl + 1) * 64, bkL, gl:32:2]),
                      [pbank[bkL]], [b_prm])
        for gl in range(2):
            P.dve(lambda e, gl=gl: e.tensor_copy(out=prm[gl * 64:(gl + 1) * 64, 2, :], in_=ldtb[gl * 64:(gl + 1) * 64, gl:32:2]), [b_ldtb], [b_prm])
        def tt(out, a, b, op, r=S_, w=S_):
            return V(lambda e: e.tensor_tensor(out=out, in0=a, in1=b, op=op), r, w)

        def act(out, in_, func, r=S_, w=S_, **kw):
            return P.act(lambda e: e.activation(out=out, in_=in_, func=func, **kw), r, w)

        act(prmv(2), prmv(2), AF.Exp)
        tt(prmv(7), prmv(0), prmv(2), ALU.mult)
        act(prmv(3), prmv(7), AF.Exp)
        tt(prmv(4), prmv(1), prmv(2), ALU.mult)
        halfpi, b_hp = A("halfpi", [128, 1], F32)
        V(lambda e: e.memset(halfpi, math.pi / 2), [], [b_hp])
        act(prmv(6), prmv(4), AF.Sin, scale=1.0 / 16)
        act(prmv(5), prmv(4), AF.Sin, r=S_ + [b_hp], scale=1.0 / 16, bias=halfpi)
        for _ in range(4):
            tt(prmv(7), prmv(5), prmv(5), ALU.mult)
            tt(prmv(8), prmv(6), prmv(6), ALU.mult)
            V(lambda e: e.scalar_tensor_tensor(out=prmv(6), in0=prmv(5), scalar=2.0, in1=prmv(6), op0=ALU.mult, op1=ALU.mult), S_, S_)
            tt(prmv(5), prmv(7), prmv(8), ALU.subtract)
        tt(prmv(9), prmv(3), prmv(5), ALU.mult)
        tt(prmv(10), prmv(3), prmv(6), ALU.mult)
        tt(prmv(7), prmv(0), prmv(0), ALU.mult)
        tt(prmv(8), prmv(1), prmv(1), ALU.mult)
        tt(prmv(11), prmv(7), prmv(8), ALU.add)
        V(lambda e: e.reciprocal(out=prmv(11), in_=prmv(11)), S_, S_)
        V(lambda e: e.tensor_scalar_add(out=prmv(12), in0=prmv(9), scalar1=-1.0), S_, S_)
        tt(prmv(7), prmv(12), prmv(0), ALU.mult)
        tt(prmv(8), prmv(10), prmv(1), ALU.mult)
        tt(prmv(7), prmv(7), prmv(8), ALU.add)
        tt(prmv(13), prmv(7), prmv(11), ALU.mult)
        tt(prmv(7), prmv(10), prmv(0), ALU.mult)
        tt(prmv(8), prmv(12), prmv(1), ALU.mult)
        tt(prmv(7), prmv(7), prmv(8), ALU.subtract)
        tt(prmv(14), prmv(7), prmv(11), ALU.mult)
        SP_ = S_ + [b_PW]
        V(lambda e: e.memset(PW[:, 0, 0, :], 1.0), [], [b_PW])
        V(lambda e: e.memset(PW[:, 1, 0, :], 0.0), [], [b_PW])
        for m in range(8):
            tt(prmv(7), PW[:, 0, m, :], prmv(9), ALU.mult, SP_, SP_)
            tt(prmv(8), PW[:, 1, m, :], prmv(10), ALU.mult, SP_, SP_)
            tt(PW[:, 0, m + 1, :], prmv(7), prmv(8), ALU.subtract, SP_, SP_)
            tt(prmv(7), PW[:, 0, m, :], prmv(10), ALU.mult, SP_, SP_)
            tt(prmv(8), PW[:, 1, m, :], prmv(9), ALU.mult, SP_, SP_)
            tt(PW[:, 1, m + 1, :], prmv(7), prmv(8), ALU.add, SP_, SP_)
        ST_ = SP_ + [b_tb]
        V(lambda e: e.tensor_copy(out=tb[:, 0, :], in_=PW[:, 0, 1, :]), ST_, ST_)
        V(lambda e: e.tensor_copy(out=tb[:, 1, :], in_=PW[:, 1, 1, :]), ST_, ST_)
        tt(prmv(7), PW[:, 0, 8, :], PW[:, 0, 8, :], ALU.mult, ST_, ST_)
        tt(prmv(8), PW[:, 1, 8, :], PW[:, 1, 8, :], ALU.mult, ST_, ST_)
        tt(prmv(7), prmv(7), prmv(8), ALU.add, ST_, ST_)
        act(tb[:, 4, :], prmv(7), AF.Sqrt, r=ST_, w=ST_)
        V(lambda e: e.reciprocal(out=prmv(8), in_=tb[:, 4, :]), ST_, ST_)
        tt(tb[:, 2, :], PW[:, 0, 8, :], prmv(8), ALU.mult, ST_, ST_)
        tt(tb[:, 3, :], PW[:, 1, 8, :], prmv(8), ALU.mult, ST_, ST_)
        V(lambda e: e.memset(tb[:, 5:7, :], 0.0), ST_, ST_)
        for ri in range(2):
            bk = bank()
            for c4 in range(4):
                P.pe(lambda e, ri=ri, c4=c4, bk=bk: e.transpose(out=psum[:, bk, c4 * 128:(c4 + 1) * 128],
                                                                in_=Cn[:, ri, c4, :, :].rearrange("p a b -> p (a b)"), identity=identf),
                     [b_Cn, b_identf], [pbank[bk]])
            pv = psum[:, bk, :].rearrange("p (c q g h) -> p c q g h", c=4, q=4, g=2)
            for c4 in range(4):
                V(lambda e, ri=ri, c4=c4, pv=pv: e.tensor_copy(out=Cb[0:64, ri, 4 * c4:4 * c4 + 4, 0:16], in_=pv[0:64, c4, :, 0, :]),
                  [pbank[bk]], [b_Cb])
                V(lambda e, ri=ri, c4=c4, pv=pv: e.tensor_copy(out=Cb[64:128, ri, 4 * c4:4 * c4 + 4, 16:32], in_=pv[64:128, c4, :, 1, :]),
                  [pbank[bk]], [b_Cb])

        emit_cols()
        SB_ = S_ + [b_Bb, b_KBk, b_T1, b_T2]
        t1k = T1[:, 0]; t2k = T2[:, 0]

        def bc(v):
            return v.unsqueeze(2).broadcast_to([128, 16, 32])

        tt(t1k, Bb[:, 0], bc(prmv(13)), ALU.mult, SB_, SB_)
        tt(t2k, Bb[:, 1], bc(prmv(14)), ALU.mult, SB_, SB_)
        tt(KBk[:, 0], t1k, t2k, ALU.subtract, SB_, SB_)
        tt(t1k, Bb[:, 0], bc(prmv(14)), ALU.mult, SB_, SB_)
        tt(t2k, Bb[:, 1], bc(prmv(13)), ALU.mult, SB_, SB_)
        tt(KBk[:, 1], t1k, t2k, ALU.add, SB_, SB_)

        def cmul_all(eng, out_re, out_im, src, b_src, m0, nm, ta, tb_, b_ta, b_tb_, b_out, neg_im):
            sre = src[:, 0].unsqueeze(1).broadcast_to([128, nm, 16, 32])
            sim = src[:, 1].unsqueeze(1).broadcast_to([128, nm, 16, 32])
            pr = PW[:, 0, m0:m0 + nm, :].unsqueeze(3).broadcast_to([128, nm, 16, 32])
            pi_ = PW[:, 1, m0:m0 + nm, :].unsqueeze(3).broadcast_to([128, nm, 16, 32])
            ta = ta[:, 0:nm]; tb_ = tb_[:, 0:nm]
            rd = [b_src, b_PW, b_ta, b_tb_]
            P.op(eng, lambda e: e.tensor_tensor(out=ta, in0=sre, in1=pr, op=ALU.mult), rd, [b_ta])
            P.op(eng, lambda e: e.tensor_tensor(out=tb_, in0=sim, in1=pi_, op=ALU.mult), rd, [b_tb_])
            P.op(eng, lambda e: e.tensor_tensor(out=out_re, in0=ta, in1=tb_, op=ALU.subtract), [b_ta, b_tb_], [b_out])
            P.op(eng, lambda e: e.tensor_tensor(out=ta, in0=sre, in1=pi_, op=ALU.mult), rd, [b_ta])
            P.op(eng, lambda e: e.tensor_tensor(out=tb_, in0=sim, in1=pr, op=ALU.mult), rd, [b_tb_])
            if neg_im:
                P.op(eng, lambda e: e.tensor_tensor(out=ta, in0=ta, in1=tb_, op=ALU.add), [b_ta, b_tb_], [b_ta])
                P.op(eng, lambda e: e.tensor_scalar_mul(out=out_im, in0=ta, scalar1=-1.0), [b_ta], [b_out])
            else:
                P.op(eng, lambda e: e.tensor_tensor(out=out_im, in0=ta, in1=tb_, op=ALU.add), [b_ta, b_tb_], [b_out])

        for m0 in (0, 4):
            cmul_all("dve", ABa[:, 0, m0:m0 + 4], ABa[:, 1, m0:m0 + 4], KBk, b_KBk, m0, 4, T1, T2, b_T1, b_T2, b_ABa, False)
        for (m0, nm) in ((0, 2), (2, 2), (4, 2), (6, 2), (8, 1)):
            cmul_all("dve", W3f[:, m0:m0 + nm, 0], W3f[:, m0:m0 + nm, 1], Cb, b_Cb, m0, nm, U1, U2, b_U1, b_U2, b_W3, True)
        for ri in range(2):
            for ql in range(4):
                V(lambda e, ri=ri, ql=ql: e.tensor_copy(out=Bpad[:, ri, ql::4, 32 * ql:32 * ql + 32], in_=ABa[:, ri, 0, ql::4, :]),
                  [b_ABa], [b_Bpad])
        for m in range(8):
            bk = bank()
            pbf = psum[:, bk, :].bitcast(BF16)
            for ri in range(2):
                for cc in range(4):
                    P.pe(lambda e, ri=ri, cc=cc, m=m, pbf=pbf: e.transpose(out=pbf[:, (ri * 4 + cc) * 128:(ri * 4 + cc + 1) * 128],
                                                                          in_=ABa[:, ri, m, 4 * cc:4 * cc + 4, :].rearrange("p a b -> p (a b)"), identity=identb),
                         [b_ABa, b_identb], [pbank[bk]])
            if m % 2 == 0:
                P.act(lambda e, m=m, pbf=pbf: e.activation(out=W1[:, :, m, :, :], in_=pbf.rearrange("p (r c n) -> p c r n", r=2, c=4), func=AF.Copy),
                      [pbank[bk]], [b_W1])
            else:
                V(lambda e, m=m, pbf=pbf: e.tensor_copy(out=W1[:, :, m, :, :], in_=pbf.rearrange("p (r c n) -> p c r n", r=2, c=4)),
                  [pbank[bk]], [b_W1])
        for cc in range(4):
            for half in range(2):
                bk = bank()
                for t4 in range(4):
                    tau = half * 4 + t4
                    for ql in range(4):
                        for ri in range(2):
                            P.pe(lambda e, cc=cc, tau=tau, ql=ql, ri=ri, bk=bk, t4=t4: e.matmul(
                                out=psum[:, bk, t4 * 128 + 32 * ql:t4 * 128 + 32 * ql + 32],
                                lhsT=Bpad[:, ri, 4 * cc + ql, :], rhs=W3f[:, tau, ri, 4 * cc + ql, :],
                                start=(ri == 0), stop=(ri == 1)), [b_Bpad, b_W3], [pbank[bk]])
                if half == 0:
                    V(lambda e, cc=cc, bk=bk: e.scalar_tensor_tensor(out=KB_[:, cc, 0, :], in0=identf, scalar=cols[:, 4, cc:cc + 1],
                                                                     in1=psum[:, bk, 0:128], op0=ALU.mult, op1=ALU.add),
                      [pbank[bk], b_identf, b_cols], [b_KB])
                    V(lambda e, cc=cc, bk=bk: e.tensor_copy(out=KB_[:, cc, 1:4, :], in_=psum[:, bk, 128:512].rearrange("p (t n) -> p t n", n=128)),
                      [pbank[bk]], [b_KB])
                else:
                    V(lambda e, cc=cc, bk=bk: e.tensor_copy(out=KB_[:, cc, 4:8, :], in_=psum[:, bk, :].rearrange("p (t n) -> p t n", n=128)),
                      [pbank[bk]], [b_KB])
        E1, b_E1 = A("E1s", [128, 16, 32], F32)
        E2, b_E2 = A("E2s", [128, 16, 32], F32)
        SE_ = [b_prm, b_tb, b_ER, b_EI, b_E1, b_E2]

        def pt(out, a, b, op):
            return P.pool(lambda e: e.tensor_tensor(out=out, in0=a, in1=b, op=op), SE_, SE_)

        P.pool(lambda e: e.memset(ER[:, :, 0:1], 1.0), [], [b_ER])
        P.pool(lambda e: e.memset(EI[:, :, 0:1], 0.0), [], [b_EI])
        P.pool(lambda e: e.tensor_copy(out=prm[:, 15:17, :], in_=tb[:, 2:4, :]), SE_, SE_)
        for k in range(6):
            n = 1 << k
            pr = prm[:, 15, :].unsqueeze(2).broadcast_to([128, 16, n])
            pi_ = prm[:, 16, :].unsqueeze(2).broadcast_to([128, 16, n])
            pt(E1[:, :, 0:n], ER[:, :, 0:n], pr, ALU.mult)
            pt(E2[:, :, 0:n], EI[:, :, 0:n], pi_, ALU.mult)
            pt(ER[:, :, n:2 * n], E1[:, :, 0:n], E2[:, :, 0:n], ALU.subtract)
            pt(E1[:, :, 0:n], ER[:, :, 0:n], pi_, ALU.mult)
            pt(E2[:, :, 0:n], EI[:, :, 0:n], pr, ALU.mult)
            pt(EI[:, :, n:2 * n], E1[:, :, 0:n], E2[:, :, 0:n], ALU.add)
            if k < 5:
                pt(prmv(17), prmv(15), prmv(15), ALU.mult)
                pt(prmv(18), prmv(16), prmv(16), ALU.mult)
                pt(prmv(16), prmv(15), prmv(16), ALU.mult)
                P.pool(lambda e: e.tensor_scalar_mul(out=prmv(16), in0=prmv(16), scalar1=2.0), SE_, SE_)
                pt(prmv(15), prmv(17), prmv(18), ALU.subtract)
        AR_.release(markA)
        ctx = dict(nc=nc, P=P, D=D, A=A, AR_=AR_, psum=psum, pbank=pbank, bank=bank, bank2=bank2, bank4=bank4, held=held,
                   identf=identf, b_identf=b_identf, identb=identb, b_identb=b_identb, cols=cols, b_cols=b_cols,
                   W1=W1, b_W1=b_W1, W3=W3, b_W3=b_W3, KB_=KB_, b_KB=b_KB, ER=ER, b_ER=b_ER, EI=EI, b_EI=b_EI,
                   tb=tb, b_tb=b_tb, poolcorr=poolcorr, b_pc=b_pc, w_in=w_in, b_win=b_win, wgv=wgv, b_wgv=b_wgv,
                   wgg=wgg, b_wgg=b_wgg, wpool=wpool, b_wpool=b_wpool, x1d=x1d, x1buf=x1buf, outbuf=outbuf,
                   mark0=mark0, markA=markA)
        phase_a(ctx)
        AR_.release(mark0)
        phase_b(ctx)

        sems = {}
        sems["eng"] = {e: es.enter_context(nc.semaphore("s_" + e)) for e in ENGS}
        sems["dma"] = [es.enter_context(nc.semaphore("d%d" % i)) for i in range(N_DMA_SEMS)]
        block = es.enter_context(nc.Block())
        run = P.emit(sems)

        @block.tensor
        def _(e):
            run("pe", e)

        @block.scalar
        def _(e):
            run("act", e)

        @block.vector
        def _(e):
            run("dve", e)

        @block.gpsimd
        def _(e):
            run("pool", e)

        @block.sync
        def _(e):
            run("sp", e)
    return nc


def _norm_front(c, xt, b_xt, rows, slot):
    P = c["P"]
    xn, b_xn, ss, b_ss = slot
    P.act(lambda e: e.activation(out=xn[0:rows, :], in_=xt, func=AF.Square, accum_out=ss[0:rows, 0:1]), [b_xt], [b_xn, b_ss])
    P.dve(lambda e: e.tensor_scalar(out=ss[0:rows, 1:2], in0=ss[0:rows, 0:1], scalar1=1.0 / DM, scalar2=1e-6, op0=ALU.mult, op1=ALU.add), [b_ss], [b_ss])
    P.act(lambda e: e.activation(out=ss[0:rows, 1:2], in_=ss[0:rows, 1:2], func=AF.Sqrt), [b_ss], [b_ss])
    P.dve(lambda e: e.reciprocal(out=ss[0:rows, 2:3], in_=ss[0:rows, 1:2]), [b_ss], [b_ss])
    P.act(lambda e: e.activation(out=xn[0:rows, :], in_=xt, func=AF.Copy, scale=ss[0:rows, 2:3]), [b_xt, b_ss], [b_xn])


def _norm_back(c, rows, gi, hT, b_hT, col0, slot):
    P = c["P"]; psum = c["psum"]; pbank = c["pbank"]; cols = c["cols"]; identb = c["identb"]
    xn, b_xn, ss, b_ss = slot
    bk = c["bank"]()
    pbf = psum[:, bk, :].bitcast(BF16)
    for k in range(8):
        P.pe(lambda e, k=k: e.transpose(out=pbf[:, k * 128:k * 128 + rows], in_=xn[0:rows, k * 128:(k + 1) * 128],
                                        identity=identb[0:rows, 0:rows]), [b_xn, c["b_identb"]], [pbank[bk]])
    P.dve(lambda e: e.tensor_tensor(out=hT[:, 0:8, col0:col0 + rows],
                                    in0=pbf.rearrange("p (k n) -> p k n", n=128)[:, :, 0:rows],
                                    in1=cols[:, gi, 0:8].unsqueeze(2).broadcast_to([128, 8, rows]), op=ALU.mult),
          [pbank[bk], c["b_cols"]], [b_hT])


def _norm_T(c, xt, b_xt, rows, gi, hT, b_hT, col0, slot):
    _norm_front(c, xt, b_xt, rows, slot)
    _norm_back(c, rows, gi, hT, b_hT, col0, slot)


def phase_a(c):
    P = c["P"]; A = c["A"]; D = c["D"]; psum = c["psum"]; pbank = c["pbank"]; bank = c["bank"]; bank2 = c["bank2"]
    cols = c["cols"]; b_cols = c["b_cols"]; identf = c["identf"]; b_identf = c["b_identf"]
    W1 = c["W1"]; b_W1 = c["b_W1"]; W3 = c["W3"]; b_W3 = c["b_W3"]; KB_ = c["KB_"]; b_KB = c["b_KB"]
    ER = c["ER"]; EI = c["EI"]; b_ER = c["b_ER"]; b_EI = c["b_EI"]; tb = c["tb"]; b_tb = c["b_tb"]
    w_in = c["w_in"]; b_win = c["b_win"]; wgv = c["wgv"]; wgg = c["wgg"]; b_wgv = c["b_wgv"]; b_wgg = c["b_wgg"]
    wpool = c["wpool"]; b_wpool = c["b_wpool"]; x1d = c["x1d"]; x1buf = c["x1buf"]; outbuf = c["outbuf"]
    w_out, b_wout = A("w_out", [128, 8, DM], BF16)
    for k in range(8):
        P.dma("pool", w_out[:, k, :], D["w_out"][k * 128:(k + 1) * 128, :], writes=[b_wout], group=True, max_dma_last_dim=4096)
    xts = [A("xtA%d" % i, [128, DM], F32) for i in range(2)]
    slots = []
    for i in range(2):
        xn_, b_xn_ = A("xnA%d" % i, [128, DM], BF16)
        ss_, b_ss_ = A("ssA%d" % i, [128, 4], F32)
        slots.append((xn_, b_xn_, ss_, b_ss_))
    nslot = [0]
    hT, b_hT = A("hTA", [128, 8, TOK], BF16)
    L = 15 + TOK
    ua, b_ua = A("ua", [128, 4, L], F32)
    pa, b_pa = A("pa", [128, L], F32)
    pb, b_pb = A("pb", [128, L], F32)
    pooled, b_pooled = A("pooled", [128, 4, TOK], BF16)
    ub, b_ub = A("ub", [128, 4, TOK], BF16)
    gl, b_gl = A("gl", [128, 4, TOK], BF16)
    mgf, b_mg = A("mg", [128, 4 * TOK], F32)
    mg = mgf.bitcast(BF16).rearrange("p (k n) -> p k n", n=TOK)
    sblk, b_sblk = A("sblk", [128, 5 * TOK], F32)
    _sub = []
    _soff = c["AR_"].live[-1][0]
    for i in range(5):
        bb = Buf("sblk%d" % i)
        bb.readers = list(b_sblk.readers)
        _sub.append((sblk[:, i * TOK:(i + 1) * TOK], bb))
        c["AR_"].live.append((_soff + i * TOK * 4, TOK * 4, bb))
    (sa, b_sa), (sg, b_sg), (sb, b_sb), (sa2, b_sa2), (sg2, b_sg2) = _sub
    rslots = [(sblk[:, 0:2 * TOK], [b_sa, b_sg]), (sblk[:, 2 * TOK:4 * TOK], [b_sb, b_sa2])]
    rcnt = [0]
    sb2, b_sb2 = pa[:, 0:TOK], b_pa
    mbufs = [(sa, b_sa, sg, b_sg, sb, b_sb), (sa2, b_sa2, sg2, b_sg2, sb2, b_sb2)]
    Gt, b_Gt = A("Gt", [128, 2, 4, 64], F32)
    Tm, b_Tm = A("Tm", [128, 2, 4, 64], F32)
    St, b_St = A("St", [128, 2, 4, 64], F32)
    SPb, b_SPb = A("SPb", [128, 2, 16, 65], BF16)
    osm, b_osm = mgf[0:16, :], b_mg
    NCH = TOK // TCH
    xcnt = [0]

    def load_x(src):
        i = xcnt[0] % 2
        xcnt[0] += 1
        xt, b_xt = xts[i]
        return xt, b_xt

    fslot = {}

    def front_only(key, src_rows, rows, q="pool"):
        xt, b_xt = load_x(None)
        P.dma(q, xt[0:rows, :], src_rows, writes=[b_xt])
        slot = slots[nslot[0] % 2]
        nslot[0] += 1
        _norm_front(c, xt[0:rows, :], b_xt, rows, slot)
        fslot[key] = slot

    def back_only(key, rows, col0):
        _norm_back(c, rows, 0, hT, b_hT, col0, fslot.pop(key))

    def dense_front(src_rows, rows, col0):
        front_only(("tmp", col0), src_rows, rows)
        back_only(("tmp", col0), rows, col0)

    def win_uaub(ntok):
        for oc in range(8):
            bk = bank()
            for k in range(8):
                P.pe(lambda e, oc=oc, k=k, bk=bk: e.matmul(out=psum[:, bk, 0:ntok], lhsT=w_in[:, k, oc * 128:(oc + 1) * 128],
                                                           rhs=hT[:, k, 0:ntok], start=(k == 0), stop=(k == 7)),
                     [b_win, b_hT], [pbank[bk]])
            if oc < 4:
                P.act(lambda e, oc=oc, bk=bk: e.activation(out=ua[:, oc, 15:15 + ntok], in_=psum[:, bk, 0:ntok], func=AF.Copy),
                      [pbank[bk]], [b_ua])
            else:
                if ntok == TOK:
                    P.act(lambda e, oc=oc, bk=bk: e.activation(out=ub[:, oc - 4, :].rearrange("p (i c) -> p i c", i=TCH),
                                                               in_=psum[:, bk, :].rearrange("p (c i) -> p i c", i=TCH), func=AF.Copy),
                          [pbank[bk]], [b_ub])
                else:
                    P.act(lambda e, oc=oc, bk=bk: e.activation(out=ub[:, oc - 4, 0:ntok], in_=psum[:, bk, 0:ntok], func=AF.Copy),
                          [pbank[bk]], [b_ub])

    half_ctr = [0, 0]

    def bank_other(bkG):
        base = (bkG + 4) % 8
        b = base + half_ctr[0] % 4
        half_ctr[0] += 1
        return b

    def gate_a_prepass(ntok, bkG, dcs=range(8)):
        for dc in dcs:
            sa, b_sa, sg, b_sg, sb, b_sb = mbufs[dc % 2]
            bk = bank_other(bkG)
            for k in range(8):
                P.pe(lambda e, dc=dc, k=k, bk=bk: e.matmul(out=psum[:, bk, 0:ntok], lhsT=w_in[:, k, 1024 + dc * 128:1024 + (dc + 1) * 128],
                                                           rhs=hT[:, k, 0:ntok], start=(k == 0), stop=(k == 7)), [b_win, b_hT], [pbank[bk]])
            P.act(lambda e, bk=bk, sa=sa: e.activation(out=sa[:, 0:ntok], in_=psum[:, bk, 0:ntok], func=AF.Sigmoid), [pbank[bk]], [b_sa])
            bk = bank_other(bkG)
            P.pe(lambda e, dc=dc, bk=bk: e.matmul(out=psum[:, bk, 0:ntok], lhsT=wpool[:, dc // 2, (dc % 2) * 128:(dc % 2) * 128 + 128],
                                                  rhs=pooled[:, dc // 2, 0:ntok], start=True, stop=True), [b_wpool, b_pooled], [pbank[bk]])
            P.dve(lambda e, dc=dc, bk=bk, sa=sa: e.scalar_tensor_tensor(out=mg[:, dc, 0:ntok], in0=psum[:, bk, 0:ntok], scalar=cols[:, 3, dc:dc + 1],
                                                                        in1=sa[:, 0:ntok], op0=ALU.mult, op1=ALU.mult), [pbank[bk], b_cols, b_sa], [b_mg])

    def merge_and_out(ntok, subtiles, x_rows_of, x1_rows_of, x1b_of, prefetch=None, pre=False):
        nx = prefetch
        if nx is not None:
            nx["f01"]()
        for dc in range(8):
            sa, b_sa, sg, b_sg, sb, b_sb = mbufs[dc % 2]
            if not pre:
                bk = bank()
                for k in range(8):
                    P.pe(lambda e, dc=dc, k=k, bk=bk: e.matmul(out=psum[:, bk, 0:ntok], lhsT=w_in[:, k, 1024 + dc * 128:1024 + (dc + 1) * 128],
                                                               rhs=hT[:, k, 0:ntok], start=(k == 0), stop=(k == 7)), [b_win, b_hT], [pbank[bk]])
                P.act(lambda e, bk=bk, sa=sa: e.activation(out=sa[:, 0:ntok], in_=psum[:, bk, 0:ntok], func=AF.Sigmoid), [pbank[bk]], [b_sa])
                bk = bank()
                P.pe(lambda e, dc=dc, bk=bk: e.matmul(out=psum[:, bk, 0:ntok], lhsT=wpool[:, dc // 2, (dc % 2) * 128:(dc % 2) * 128 + 128],
                                                      rhs=pooled[:, dc // 2, 0:ntok], start=True, stop=True), [b_wpool, b_pooled], [pbank[bk]])
                P.dve(lambda e, dc=dc, bk=bk, sa=sa: e.scalar_tensor_tensor(out=sa[:, 0:ntok], in0=psum[:, bk, 0:ntok], scalar=cols[:, 3, dc:dc + 1],
                                                                     in1=sa[:, 0:ntok], op0=ALU.mult, op1=ALU.mult), [pbank[bk], b_cols, b_sa], [b_sa])
            bk = bank()
            for k in range(8):
                P.pe(lambda e, dc=dc, k=k, bk=bk: e.matmul(out=psum[:, bk, 0:ntok], lhsT=w_in[:, k, 2048 + dc * 128:2048 + (dc + 1) * 128],
                                                           rhs=hT[:, k, 0:ntok], start=(k == 0), stop=(k == 7)), [b_win, b_hT], [pbank[bk]])
            P.act(lambda e, bk=bk, sb=sb: e.activation(out=sb[:, 0:ntok], in_=psum[:, bk, 0:ntok], func=AF.Sigmoid), [pbank[bk]], [b_sb])
            bk = bank()
            for k in range(4):
                P.pe(lambda e, dc=dc, k=k, bk=bk: e.matmul(out=psum[:, bk, 0:ntok], lhsT=wgg[:, k, dc * 128:(dc + 1) * 128],
                                                           rhs=gl[:, k, 0:ntok], start=(k == 0), stop=(k == 3)), [b_wgg, b_gl], [pbank[bk]])
            P.act(lambda e, bk=bk, sg=sg: e.activation(out=sg[:, 0:ntok], in_=psum[:, bk, 0:ntok], func=AF.Sigmoid), [pbank[bk]], [b_sg])
            bk = bank()
            for k in range(4):
                P.pe(lambda e, dc=dc, k=k, bk=bk: e.matmul(out=psum[:, bk, 0:ntok], lhsT=wgv[:, k, dc * 128:(dc + 1) * 128],
                                                           rhs=gl[:, k, 0:ntok], start=(k == 0), stop=(k == 3)), [b_wgv, b_gl], [pbank[bk]])
            P.dve(lambda e, bk=bk, sg=sg: e.tensor_tensor(out=sg[:, 0:ntok], in0=psum[:, bk, 0:ntok], in1=sg[:, 0:ntok], op=ALU.mult),
                  [pbank[bk], b_sg], [b_sg])
            P.dve(lambda e, sg=sg, sb=sb: e.tensor_tensor(out=sg[:, 0:ntok], in0=sg[:, 0:ntok], in1=sb[:, 0:ntok], op=ALU.mult), [b_sg, b_sb], [b_sg])
            if pre:
                P.dve(lambda e, dc=dc, sg=sg: e.tensor_tensor(out=mg[:, dc, 0:ntok], in0=mg[:, dc, 0:ntok], in1=sg[:, 0:ntok], op=ALU.add), [b_mg, b_sg], [b_mg])
            else:
                P.dve(lambda e, dc=dc, sa=sa, sg=sg: e.tensor_tensor(out=mg[:, dc, 0:ntok], in0=sa[:, 0:ntok], in1=sg[:, 0:ntok], op=ALU.add), [b_sa, b_sg], [b_mg])
        def wout_mm(s, rows):
            bk = bank2()
            for half in range(2):
                for k in range(8):
                    P.pe(lambda e, s=s, rows=rows, half=half, k=k, bk=bk: e.matmul(
                        out=psum[0:rows, bk + half, :], lhsT=mg[:, k, s * 128:s * 128 + rows], rhs=w_out[:, k, half * 512:(half + 1) * 512],
                        start=(k == 0), stop=(k == 7)), [b_mg, b_wout], [pbank[bk + half]])
            return bk

        def resid_load(s, rows):
            xt, bl_xt = rslots[rcnt[0] % 2]
            rcnt[0] += 1
            P.dma("sp", xt[0:rows, :], x_rows_of(s), writes=bl_xt)
            return xt, bl_xt

        def resid_add(s, rows, bk, slot):
            xt, bl_xt = slot
            P.dve(lambda e, rows=rows, bk=bk, xt=xt: e.tensor_tensor(out=xt[0:rows, :].rearrange("p (h n) -> p h n", n=512),
                                                                      in0=xt[0:rows, :].rearrange("p (h n) -> p h n", n=512),
                                                                      in1=psum[0:rows, bk:bk + 2, :], op=ALU.add),
                  [pbank[bk], pbank[bk + 1]] + bl_xt, bl_xt)
            P.dma("sp", x1_rows_of(s), xt[0:rows, :], reads=bl_xt, writes=[x1b_of(s)])

        if len(subtiles) == 4:
            (s0, r0_), (s1, r1_), (s2, r2_), (s3, r3_) = subtiles
            l0 = resid_load(s0, r0_); l1 = resid_load(s1, r1_)
            if nx is not None:
                nx["b01"](); nx["f23"]()
            b0 = wout_mm(s0, r0_); b1 = wout_mm(s1, r1_)
            resid_add(s0, r0_, b0, l0); l2 = resid_load(s2, r2_)
            resid_add(s1, r1_, b1, l1); l3 = resid_load(s3, r3_)
            b2 = wout_mm(s2, r2_); b3 = wout_mm(s3, r3_)
            if nx is not None:
                nx["b23"]()
            resid_add(s2, r2_, b2, l2); resid_add(s3, r3_, b3, l3)
        else:
            for (s, rows) in subtiles:
                l = resid_load(s, rows)
                bk = wout_mm(s, rows)
                resid_add(s, rows, bk, l)

    P.dve(lambda e: e.memset(ua[:, :, 0:15], 0.0), [], [b_ua])
    for t in range(SEQ // TOK):
        r0 = t * TOK

        def xrows(tt, s):
            return D["x"][tt * TOK + s * 128:tt * TOK + (s + 1) * 128, :]

        if t == 0:
            front_only((0, 0), xrows(0, 0), 128, "sp")
            front_only((0, 1), xrows(0, 1), 128, "sp")
            back_only((0, 0), 128, 0)
            front_only((0, 2), xrows(0, 2), 128, "sp")
            back_only((0, 1), 128, 128)
            front_only((0, 3), xrows(0, 3), 128, "sp")
            back_only((0, 2), 128, 256)
            back_only((0, 3), 128, 384)
        win_uaub(TOK)
        for g in range(4):
            w = 2 << g
            src = ua[:, g, :]
            P.dve(lambda e, src=src: e.tensor_tensor(out=pa[:, 1:L], in0=src[:, 1:L], in1=src[:, 0:L - 1], op=ALU.add), [b_ua], [b_pa])
            cur, b_cur = pa, b_pa
            if g >= 1:
                P.dve(lambda e: e.tensor_tensor(out=pb[:, 3:L], in0=pa[:, 3:L], in1=pa[:, 1:L - 2], op=ALU.add), [b_pa], [b_pb])
                cur, b_cur = pb, b_pb
            if g >= 2:
                P.dve(lambda e: e.tensor_tensor(out=pa[:, 7:L], in0=pb[:, 7:L], in1=pb[:, 3:L - 4], op=ALU.add), [b_pb], [b_pa])
                cur, b_cur = pa, b_pa
            if g >= 3:
                P.dve(lambda e: e.tensor_tensor(out=pb[:, 15:L], in0=pa[:, 15:L], in1=pa[:, 7:L - 8], op=ALU.add), [b_pa], [b_pb])
                cur, b_cur = pb, b_pb
            if t == 0:
                P.dve(lambda e, cur=cur, g=g: e.tensor_tensor(out=cur[:, 15:31], in0=cur[:, 15:31], in1=c["poolcorr"][:, g, :], op=ALU.mult),
                      [b_cur, c["b_pc"]], [b_cur])
            P.dve(lambda e, cur=cur, g=g, w=w: e.scalar_tensor_tensor(out=pooled[:, g, :], in0=cur[:, 15:L], scalar=1.0 / w, in1=ua[:, g, 15:L],
                                                                      op0=ALU.mult, op1=ALU.subtract), [b_cur, b_ua], [b_pooled])
        if t == SEQ // TOK - 1:
            bk = bank()
            for k in range(8):
                P.pe(lambda e, k=k, bk=bk: e.matmul(out=psum[0:15, bk, :], lhsT=hT[:, k, TOK - 15:TOK], rhs=w_in[:, k, 0:512],
                                                    start=(k == 0), stop=(k == 7)), [b_hT, b_win], [pbank[bk]])
            P.act(lambda e, bk=bk: e.activation(out=osm[0:15, 0:512], in_=psum[0:15, bk, :], func=AF.Copy), [pbank[bk]], [b_osm])
            P.dma("sp", D["pool_p"], osm[0:15, 0:512], reads=[b_osm], writes=[outbuf], is_output=True)
        else:
            P.dve(lambda e: e.tensor_copy(out=pa[:, 0:60].rearrange("p (g n) -> p g n", n=15), in_=ua[:, :, TOK:TOK + 15]), [b_ua], [b_pa])
            P.dve(lambda e: e.tensor_copy(out=ua[:, :, 0:15], in_=pa[:, 0:60].rearrange("p (g n) -> p g n", n=15)), [b_pa], [b_ua])
        P.dve(lambda e: e.tensor_copy(out=SPb[:, :, :, 0:1], in_=tb[:, 5:7, :].unsqueeze(3)), [b_tb], [b_SPb])
        TB = [b_tb]
        for (o_, a1, b1, a2, b2, op) in ((tb[:, 7, :], tb[:, 2, :], tb[:, 5, :], tb[:, 3, :], tb[:, 6, :], ALU.subtract),
                                         (tb[:, 8, :], tb[:, 2, :], tb[:, 6, :], tb[:, 3, :], tb[:, 5, :], ALU.add)):
            P.dve(lambda e, a1=a1, b1=b1: e.tensor_tensor(out=tb[:, 9, :], in0=a1, in1=b1, op=ALU.mult), TB, TB)
            P.dve(lambda e, a2=a2, b2=b2: e.tensor_tensor(out=tb[:, 10, :], in0=a2, in1=b2, op=ALU.mult), TB, TB)
            P.dve(lambda e, o_=o_, op=op: e.tensor_tensor(out=o_, in0=tb[:, 9, :], in1=tb[:, 10, :], op=op), TB, TB)
        bkG = c["bank4"]()
        pGb = [[Buf("pG%d_%d" % (ql, cc)) for cc in range(4)] for ql in range(4)]
        for cc in range(4):
            for ri in range(2):
                for i in range(TCH):
                    for ql in range(4):
                        P.pe(lambda e, ql=ql, ri=ri, i=i, cc=cc, bkG=bkG: e.matmul(
                            out=psum[:, bkG + ql, cc * 128 + ri * 64:cc * 128 + (ri + 1) * 64], lhsT=W1[32 * ql:32 * ql + 32, cc, TCH - 1 - i, ri, :],
                            rhs=ub[32 * ql:32 * ql + 32, cc, i * NCH:(i + 1) * NCH], start=(i == 0), stop=(i == TCH - 1), tile_position=(32 * ql, 0)),
                             [b_W1, b_ub], [pbank[bkG + ql]])
        for cc in range(4):
            q0 = 4 * cc
            bk = bkG
            pG = psum[:, bk:bk + 4, cc * 128:(cc + 1) * 128].rearrange("p q (r n) -> p r q n", r=2)
            er = ER[:, q0:q0 + 4, :]; ei = EI[:, q0:q0 + 4, :]
            RB = [pbank[bk], pbank[bk + 1], pbank[bk + 2], pbank[bk + 3], b_ER, b_EI]
            erb = er.unsqueeze(1).broadcast_to([128, 2, 4, NCH])
            eib = ei.unsqueeze(1).broadcast_to([128, 2, 4, NCH])
            P.dve(lambda e, pG=pG, erb=erb: e.tensor_tensor(out=Tm, in0=pG, in1=erb, op=ALU.mult), RB, [b_Tm])
            P.dve(lambda e, pG=pG, eib=eib: e.tensor_tensor(out=St, in0=pG, in1=eib, op=ALU.mult), RB, [b_St])
            P.dve(lambda e: e.tensor_tensor(out=Gt[:, 0], in0=Tm[:, 0], in1=St[:, 1], op=ALU.add), [b_Tm, b_St], [b_Gt])
            P.dve(lambda e: e.tensor_tensor(out=Gt[:, 1], in0=Tm[:, 1], in1=St[:, 0], op=ALU.subtract), [b_Tm, b_St], [b_Gt])
            for ql in range(4):
                for ri in range(2):
                    P.dve(lambda e, ql=ql, ri=ri, q0=q0: e.tensor_tensor_scan(
                        out=St[:, ri, ql, :], data0=tb[:, 4, q0 + ql:q0 + ql + 1].broadcast_to([128, NCH]), data1=Gt[:, ri, ql, :],
                        initial=tb[:, 7 + ri, q0 + ql:q0 + ql + 1], op0=ALU.mult, op1=ALU.add), [b_tb, b_Gt], [b_St])
            P.dve(lambda e, erb=erb: e.tensor_tensor(out=Tm, in0=St, in1=erb, op=ALU.mult), [b_St, b_ER], [b_Tm])
            P.dve(lambda e, eib=eib: e.tensor_tensor(out=Gt, in0=St, in1=eib, op=ALU.mult), [b_St, b_EI], [b_Gt])
            P.dve(lambda e, q0=q0: e.tensor_tensor(out=SPb[:, 0, q0:q0 + 4, 1:65], in0=Tm[:, 0], in1=Gt[:, 1], op=ALU.subtract), [b_Tm, b_Gt], [b_SPb])
            P.dve(lambda e, q0=q0: e.tensor_tensor(out=tb[:, 5, q0:q0 + 4], in0=Tm[:, 0, :, 63], in1=Gt[:, 1, :, 63], op=ALU.subtract), [b_Tm, b_Gt], [b_tb])
            P.dve(lambda e, q0=q0: e.tensor_tensor(out=SPb[:, 1, q0:q0 + 4, 1:65], in0=Tm[:, 1], in1=Gt[:, 0], op=ALU.add), [b_Tm, b_Gt], [b_SPb])
            P.dve(lambda e, q0=q0: e.tensor_tensor(out=tb[:, 6, q0:q0 + 4], in0=Tm[:, 1, :, 63], in1=Gt[:, 0, :, 63], op=ALU.add), [b_Tm, b_Gt], [b_tb])
            gate_a_prepass(TOK, bkG, (2 * cc, 2 * cc + 1))
            bk = bank_other(bkG)
            for j in range(TCH):
                for i in range(j + 1):
                    P.pe(lambda e, j=j, i=i, cc=cc, bk=bk: e.matmul(
                        out=psum[:, bk, j * NCH:(j + 1) * NCH], lhsT=KB_[:, cc, j - i, :], rhs=ub[:, cc, i * NCH:(i + 1) * NCH], start=(j == 0 and i == 0), stop=False,
                        skip_group_check=True), [b_KB, b_ub], [pbank[bk]])
            for j in range(TCH):
                for ri in range(2):
                    for ql in range(4):
                        P.pe(lambda e, j=j, ql=ql, ri=ri, q0=q0, bk=bk: e.matmul(
                            out=psum[32 * ql:32 * ql + 32, bk, j * NCH:(j + 1) * NCH], lhsT=W3[:, j, ri, q0 + ql, :], rhs=SPb[:, ri, q0 + ql, 0:NCH],
                            start=False, stop=(ri == 1), skip_group_check=True, tile_position=(0, 32 * ql)), [b_W3, b_SPb], [pbank[bk]])
            P.act(lambda e, cc=cc, bk=bk: e.activation(out=gl[:, cc, :].rearrange("p (c j) -> p j c", j=TCH),
                                                       in_=psum[:, bk, :].rearrange("p (j c) -> p j c", j=TCH), func=AF.Gelu_apprx_tanh), [pbank[bk]], [b_gl])
        if t == SEQ // TOK - 1:
            with c["nc"].allow_non_contiguous_dma(reason="final S5 state, 2048 elems"):
                P.dma("sp", D["re_p"].rearrange("(q g) p -> (g p) q", g=2), tb[:, 5, :], reads=[b_tb], writes=[outbuf], is_output=True, allow_slow_non_contiguous=True)
                P.dma("sp", D["im_p"].rearrange("(q g) p -> (g p) q", g=2), tb[:, 6, :], reads=[b_tb], writes=[outbuf], is_output=True, allow_slow_non_contiguous=True)
        if t + 1 < SEQ // TOK:
            def _f01(t=t):
                front_only((t + 1, 0), xrows(t + 1, 0), 128)
                front_only((t + 1, 1), xrows(t + 1, 1), 128)

            def _b01(t=t):
                back_only((t + 1, 0), 128, 0)
                back_only((t + 1, 1), 128, 128)

            def _f23(t=t):
                front_only((t + 1, 2), xrows(t + 1, 2), 128)
                front_only((t + 1, 3), xrows(t + 1, 3), 128)

            def _b23(t=t):
                back_only((t + 1, 2), 128, 256)
                back_only((t + 1, 3), 128, 384)
            pf = {"f01": _f01, "b01": _b01, "f23": _f23, "b23": _b23}
        else:
            pf = {"f01": (lambda: front_only("samp", D["xs"], NSAMP)), "b01": (lambda: None), "f23": (lambda: None), "b23": (lambda: None)}
        merge_and_out(TOK, [(s, 128) for s in range(4)],
                      lambda s, r0=r0: D["x"][r0 + s * 128:r0 + (s + 1) * 128, :],
                      lambda s, r0=r0: x1d[r0 + s * 128:r0 + (s + 1) * 128, :],
                      lambda s, r0=r0: x1buf[(r0 + s * 128) // 128], prefetch=pf, pre=True)

    NS = NSAMP
    back_only("samp", NS, 0)
    win_uaub(NS)
    P.dma("sp", D["pool_s"][:, 0:14, :], D["spool"][:, 1:15, :], writes=[outbuf], is_output=True)
    bk = bank()
    for k in range(8):
        P.pe(lambda e, k=k, bk=bk: e.matmul(out=psum[0:NS, bk, :], lhsT=hT[:, k, 0:NS], rhs=w_in[:, k, 0:512],
                                            start=(k == 0), stop=(k == 7)), [b_hT, b_win], [pbank[bk]])
    P.act(lambda e, bk=bk: e.activation(out=osm[0:NS, 0:512], in_=psum[0:NS, bk, :], func=AF.Copy), [pbank[bk]], [b_osm])
    P.dma("sp", D["pool_s"][:, 14, :], osm[0:NS, 0:512], reads=[b_osm], writes=[outbuf], is_output=True)
    pbuf = Tm
    sp_tok = pb[0:120, 0:512]
    bufT = pa[:, 0:480].rearrange("p (h n) -> p h n", h=4)
    for hf in range(2):
        P.dma("sp", sp_tok, D["spool"][hf * 8:(hf + 1) * 8].rearrange("s r c -> (s r) c"), writes=[b_pb])
        bk = bank()
        for g in range(4):
            P.pe(lambda e, g=g, bk=bk: e.transpose(out=psum[:, bk, g * 120:(g + 1) * 120], in_=sp_tok[:, g * 128:(g + 1) * 128],
                                                   identity=identf[0:120, 0:120]), [b_pb, b_identf], [pbank[bk]])
        P.dve(lambda e, bk=bk: e.tensor_copy(out=pa[:, 0:480], in_=psum[:, bk, 0:480]), [pbank[bk]], [b_pa])
        bv = bufT.rearrange("p g (s r) -> p g s r", r=15)
        for g in range(4):
            w = 2 << g
            P.dve(lambda e, g=g, w=w, hf=hf, bv=bv: e.tensor_reduce(out=sa[:, g * 16 + hf * 8:g * 16 + hf * 8 + 8], in_=bv[:, g, :, 15 - (w - 1):15],
                                                                    axis=mybir.AxisListType.X, op=ALU.add), [b_pa], [b_sa])
    for g in range(4):
        w = 2 << g
        P.dve(lambda e, g=g: e.tensor_tensor(out=sa[:, g * 16:g * 16 + 16], in0=sa[:, g * 16:g * 16 + 16], in1=ua[:, g, 15:15 + NS], op=ALU.add),
              [b_sa, b_ua], [b_sa])
        P.dve(lambda e, g=g, w=w: e.scalar_tensor_tensor(out=pooled[:, g, 0:NS], in0=sa[:, g * 16:g * 16 + 16], scalar=1.0 / w, in1=ua[:, g, 15:15 + NS],
                                                         op0=ALU.mult, op1=ALU.subtract), [b_sa, b_ua], [b_pooled])
    H0, b_H0 = Gt, b_Gt
    H0v = Gt.rearrange("p r q n -> p r (q n)").rearrange("p r (q s) -> p r q s", s=NS)
    HNv = St.rearrange("p r q n -> p r (q n)").rearrange("p r (q s) -> p r q s", s=NS)
    T0v = Tm.rearrange("p r q n -> p r (q n)").rearrange("p r (q s) -> p r q s", s=NS)
    for ri, nm in enumerate(["sre", "sim"]):
        P.dma("sp", osm[0:NS, :], D[nm], writes=[b_osm])
        bk = bank()
        for q in range(16):
            P.pe(lambda e, q=q, bk=bk: e.transpose(out=psum[:, bk, q * NS:(q + 1) * NS], in_=osm[0:NS, q * 128:(q + 1) * 128],
                                                   identity=identf[0:NS, 0:NS]), [b_osm, b_identf], [pbank[bk]])
        P.dve(lambda e, ri=ri, bk=bk: e.tensor_copy(out=H0v[:, ri], in_=psum[:, bk, 0:256].rearrange("p (q s) -> p q s", s=NS)), [pbank[bk]], [b_Gt])
        P.dve(lambda e, ri=ri, bk=bk: e.tensor_copy(out=SPb[:, ri, :, 0:NS], in_=psum[:, bk, 0:256].rearrange("p (q s) -> p q s", s=NS)), [pbank[bk]], [b_SPb])
    bk = c["bank4"]()
    pBU = psum[:, bk:bk + 4, 0:128].rearrange("p l (r c s) -> p r c l s", r=2, c=4)
    for q in range(16):
        cc, ql = q // 4, q % 4
        for ri in range(2):
            P.pe(lambda e, q=q, cc=cc, ql=ql, ri=ri, bk=bk: e.matmul(out=psum[:, bk + ql, ri * 64 + cc * NS:ri * 64 + (cc + 1) * NS],
                                                                    lhsT=W1[32 * ql:32 * ql + 32, cc, 0, ri, :],
                                                                    rhs=ub[32 * ql:32 * ql + 32, cc, 0:NS], start=True, stop=True, tile_position=(32 * ql, 0)),
                 [b_W1, b_ub], [pbank[bk + ql]])
    a1r = tb[:, 0, :].unsqueeze(2).broadcast_to([128, 16, NS]); a1i = tb[:, 1, :].unsqueeze(2).broadcast_to([128, 16, NS])
    SS = [b_Gt, b_St, b_Tm, b_tb, pbank[bk], pbank[bk + 1], pbank[bk + 2], pbank[bk + 3]]
    P.dve(lambda e: e.tensor_tensor(out=T0v[:, 0], in0=H0v[:, 0], in1=a1r, op=ALU.mult), SS, SS)
    P.dve(lambda e: e.tensor_tensor(out=T0v[:, 1], in0=H0v[:, 1], in1=a1i, op=ALU.mult), SS, SS)
    P.dve(lambda e: e.tensor_tensor(out=T0v[:, 0], in0=T0v[:, 0], in1=T0v[:, 1], op=ALU.subtract), SS, SS)
    P.dve(lambda e: e.tensor_tensor(out=HNv[:, 0].rearrange("p (c l) s -> p c l s", l=4), in0=T0v[:, 0].rearrange("p (c l) s -> p c l s", l=4), in1=pBU[:, 0], op=ALU.add), SS, SS)
    P.dve(lambda e: e.tensor_tensor(out=T0v[:, 0], in0=H0v[:, 1], in1=a1r, op=ALU.mult), SS, SS)
    P.dve(lambda e: e.tensor_tensor(out=T0v[:, 1], in0=H0v[:, 0], in1=a1i, op=ALU.mult), SS, SS)
    P.dve(lambda e: e.tensor_tensor(out=T0v[:, 0], in0=T0v[:, 0], in1=T0v[:, 1], op=ALU.add), SS, SS)
    P.dve(lambda e: e.tensor_tensor(out=HNv[:, 1].rearrange("p (c l) s -> p c l s", l=4), in0=T0v[:, 0].rearrange("p (c l) s -> p c l s", l=4), in1=pBU[:, 1], op=ALU.add), SS, SS)
    for ri, nm in enumerate(["re_s", "im_s"]):
        bk4 = [bank() for _ in range(4)]
        for q in range(16):
            P.pe(lambda e, q=q, ri=ri, bk4=bk4: e.transpose(out=psum[0:NS, bk4[q // 4], (q % 4) * 128:(q % 4 + 1) * 128], in_=HNv[:, ri, q, :],
                                                            identity=identf), [b_St, b_identf], [pbank[bk4[q // 4]]])
        for b4 in range(4):
            P.act(lambda e, b4=b4, bk4=bk4: e.activation(out=osm[0:NS, b4 * 512:(b4 + 1) * 512], in_=psum[0:NS, bk4[b4], :], func=AF.Copy),
                  [pbank[bk4[b4]]], [b_osm])
        P.dma("sp", D[nm], osm[0:NS, :], reads=[b_osm], writes=[outbuf], is_output=True)
    for cc in range(4):
        q0 = 4 * cc
        bk = bank()
        P.pe(lambda e, cc=cc, bk=bk: e.matmul(out=psum[:, bk, 0:NS], lhsT=KB_[:, cc, 0, :], rhs=ub[:, cc, 0:NS], start=True, stop=False),
             [b_KB, b_ub], [pbank[bk]])
        for ql in range(4):
            for ri in range(2):
                P.pe(lambda e, ql=ql, ri=ri, q0=q0, bk=bk: e.matmul(out=psum[32 * ql:32 * ql + 32, bk, 0:NS], lhsT=W3[:, 0, ri, q0 + ql, :],
                                                                    rhs=SPb[:, ri, q0 + ql, 0:NS], start=False, stop=(ri == 1),
                                                                    tile_position=(0, 32 * ql)), [b_W3, b_SPb], [pbank[bk]])
        P.act(lambda e, cc=cc, bk=bk: e.activation(out=gl[:, cc, 0:NS], in_=psum[:, bk, 0:NS], func=AF.Gelu_apprx_tanh), [pbank[bk]], [b_gl])
    merge_and_out(NS, [(0, NS)], lambda s: D["xs"], lambda s: x1d[SEQ:SEQ + NS, :], lambda s: x1buf[SEQ // 128])


def phase_b(c):
    P = c["P"]; A = c["A"]; D = c["D"]; psum = c["psum"]; pbank = c["pbank"]; bank = c["bank"]; bank2 = c["bank2"]
    cols = c["cols"]; b_cols = c["b_cols"]; identf = c["identf"]; b_identf = c["b_identf"]
    x1d = c["x1d"]; x1buf = c["x1buf"]; outbuf = c["outbuf"]; held = c["held"]
    gfin, b_gfin = A("gfin", [128, DM], F32)
    P.dma("sp", gfin, D["g_final"].partition_broadcast(128), writes=[b_gfin])
    AR_ = c["AR_"]
    off1 = AR_.top
    wf1, b_wf1 = A("wf1", [128, 8, 4096], BF16)
    bw1 = [[AR_.sub_buf("wf1_%d_%d" % (k, h), off1, (k * 4096 + h * 2048) * 2, 4096) for h in range(2)] for k in range(8)]
    for h in range(2):
        for k in range(8):
            P.dma("pool", wf1[:, k, h * 2048:(h + 1) * 2048], D["w_ff1"][k * 128:(k + 1) * 128, h * 2048:(h + 1) * 2048], writes=[bw1[k][h]], max_dma_last_dim=4096)
    off2 = AR_.top
    wf2, b_wf2 = A("wf2", [128, 32, DM], BF16)
    bw2 = [AR_.sub_buf("wf2_%d" % k4, off2, k4 * 4 * DM * 2, 4 * DM * 2) for k4 in range(8)]
    for k4 in range(8):
        P.dma("pool", wf2[:, 4 * k4:4 * k4 + 4, :], D["w_ff2"][k4 * 512:(k4 + 1) * 512, :].rearrange("(k p) n -> p k n", p=128), writes=[bw2[k4]], max_dma_last_dim=4096)
    wpg, b_wpg = A("wpg", [128, 8, DM], BF16)
    for k2 in range(4):
        P.dma("pool", wpg[:, 2 * k2:2 * k2 + 2, :], D["w_ple_gate"][k2 * 256:(k2 + 1) * 256, :].rearrange("(k p) n -> p k n", p=128), writes=[b_wpg], group=True, max_dma_last_dim=4096)
    wpl, b_wpl = A("wpl", [128, 2, DM], BF16)
    P.dma("pool", wpl, D["w_ple"].rearrange("(k p) n -> p k n", p=128), writes=[b_wpl], max_dma_last_dim=4096)
    xts = [A("xtB%d" % i, [128, DM], F32) for i in range(4)]
    slots = []
    for i in range(2):
        xn_, b_xn_ = A("xnB%d" % i, [128, DM], BF16)
        ss_, b_ss_ = A("ssB%d" % i, [128, 4], F32)
        slots.append((xn_, b_xn_, ss_, b_ss_))
    ssf = [A("ssF%d" % i, [128, 4], F32) for i in range(2)]
    hTs = [A("hTB%d" % i, [128, 8, TOKB], BF16) for i in range(2)]
    offa = AR_.top
    aTf, b_aT = A("aT", [128, 16 * TOKB], F32)
    aT = aTf.bitcast(BF16).rearrange("p (k n) -> p k n", n=TOKB)
    b_aTlo = AR_.sub_buf("aT_lo", offa, 0, 24 * TOKB * 2)
    b_aThi = AR_.sub_buf("aT_hi", offa, 24 * TOKB * 2, 8 * TOKB * 2)
    sig_hi = aTf[:, 12 * TOKB:16 * TOKB]

    def b_aT_of(fc):
        return b_aTlo if fc < 24 else b_aThi
    rts = [A("rt%d" % i, [128, TOKB], F32) for i in range(2)]
    pts = [A("pt%d" % i, [128, 256], F32) for i in range(2)]
    pTs = [A("pT%d" % i, [128, 2, 128], BF16) for i in range(2)]
    sigs = [A("sig0", [128, DM], F32), (sig_hi, b_aThi)]

    NT = SEQ // TOKB
    tiles = []
    for t in range(NT):
        r0 = t * TOKB
        tiles.append(dict(subs=[(0, 128), (1, 128)], ntok=TOKB,
                          x1rows=(lambda s, r0=r0: x1d[r0 + s * 128:r0 + (s + 1) * 128, :]),
                          x1b=(lambda s, r0=r0: x1buf[(r0 + s * 128) // 128]),
                          prows=(lambda s, r0=r0: D["p"][r0 + s * 128:r0 + (s + 1) * 128, :]),
                          yrows=(lambda s, r0=r0: D["y"][r0 + s * 128:r0 + (s + 1) * 128, :])))
    tiles.append(dict(subs=[(0, NSAMP)], ntok=NSAMP, x1rows=(lambda s: x1d[SEQ:SEQ + NSAMP, :]), x1b=(lambda s: x1buf[SEQ // 128]),
                      prows=(lambda s: D["ps"]), yrows=(lambda s: D["ys"])))

    def xt_of(t, s):
        return xts[2 * (t % 2) + s]

    def front(t):
        T = tiles[t]
        for (s, rows) in T["subs"]:
            xt, b_xt = xt_of(t, s)
            P.dma("sp", xt[0:rows, :], T["x1rows"](s), reads=[T["x1b"](s)], writes=[b_xt])
            _norm_front(c, xt[0:rows, :], b_xt, rows, slots[s])

    def back(t):
        T = tiles[t]
        hT, b_hT = hTs[t % 2]
        for (s, rows) in T["subs"]:
            _norm_back(c, rows, 1, hT, b_hT, s * 128, slots[s])

    def ffn2(t, s, rows, mid=None):
        xt, b_xt = xt_of(t, s)
        bk = bank2()
        for half in range(2):
            if half == 1 and mid is not None:
                mid()
            for fc in range(32):
                P.pe(lambda e, s=s, rows=rows, half=half, fc=fc, bk=bk: e.matmul(
                    out=psum[0:rows, bk + half, :], lhsT=aT[:, fc, s * 128:s * 128 + rows], rhs=wf2[:, fc, half * 512:(half + 1) * 512],
                    start=(fc == 0), stop=(fc == 31)), [b_aT_of(fc), bw2[fc // 4]], [pbank[bk + half]])
        P.dve(lambda e, rows=rows, bk=bk, xt=xt: e.tensor_tensor(out=xt[0:rows, :].rearrange("p (h n) -> p h n", n=512),
                                                                  in0=xt[0:rows, :].rearrange("p (h n) -> p h n", n=512),
                                                                  in1=psum[0:rows, bk:bk + 2, :], op=ALU.add),
              [pbank[bk], pbank[bk + 1], b_xt], [b_xt])

    pending = []

    def final_norm(t, subs):
        T = tiles[t]
        for (s, rows) in subs:
            xt, b_xt = xt_of(t, s)
            ss, b_ss = ssf[s]
            junk, b_junk = slots[s][0], slots[s][1]
            P.act(lambda e, rows=rows, xt=xt, junk=junk, ss=ss: e.activation(out=junk[0:rows, :], in_=xt[0:rows, :], func=AF.Square, accum_out=ss[0:rows, 0:1]),
                  [b_xt], [b_junk, b_ss])
            P.dve(lambda e, rows=rows, ss=ss: e.tensor_scalar(out=ss[0:rows, 1:2], in0=ss[0:rows, 0:1], scalar1=1.0 / DM, scalar2=1e-6, op0=ALU.mult, op1=ALU.add), [b_ss], [b_ss])
            P.act(lambda e, rows=rows, ss=ss: e.activation(out=ss[0:rows, 1:2], in_=ss[0:rows, 1:2], func=AF.Sqrt), [b_ss], [b_ss])
            P.dve(lambda e, rows=rows, ss=ss: e.reciprocal(out=ss[0:rows, 2:3], in_=ss[0:rows, 1:2]), [b_ss], [b_ss])
            P.dve(lambda e, rows=rows, xt=xt, ss=ss: e.scalar_tensor_tensor(out=xt[0:rows, :], in0=xt[0:rows, :], scalar=ss[0:rows, 2:3], in1=gfin[0:rows, :],
                                                                            op0=ALU.mult, op1=ALU.mult), [b_xt, b_ss, b_gfin], [b_xt])
            P.dma("sp", T["yrows"](s), xt[0:rows, :], reads=[b_xt], writes=[outbuf], is_output=True)

    front(0)
    back(0)
    for t in range(len(tiles)):
        T = tiles[t]
        ntok = T["ntok"]
        hT, b_hT = hTs[t % 2]
        for (s, rows) in T["subs"]:
            pt, b_pt = pts[s]
            P.dma("sp", pt[0:rows, :], T["prows"](s), writes=[b_pt])
        if ntok <= 128:
            for f4 in range(8):
                bk = bank()
                for j in range(4):
                    fc = 4 * f4 + j
                    for k in range(8):
                        P.pe(lambda e, fc=fc, j=j, k=k, bk=bk, hT=hT, ntok=ntok: e.matmul(out=psum[:, bk, j * ntok:(j + 1) * ntok], lhsT=wf1[:, k, fc * 128:(fc + 1) * 128],
                                                                                      rhs=hT[:, k, 0:ntok], start=(k == 0), stop=(k == 7)), [bw1[k][fc // 16], b_hT], [pbank[bk]])
                rt, b_rt = rts[f4 % 2]
                P.act(lambda e, bk=bk, rt=rt, ntok=ntok: e.activation(out=rt[:, 0:4 * ntok], in_=psum[:, bk, 0:4 * ntok], func=AF.Relu), [pbank[bk]], [b_rt])
                P.dve(lambda e, f4=f4, bk=bk, rt=rt, ntok=ntok: e.tensor_tensor(out=aT[:, 4 * f4:4 * f4 + 4, 0:ntok],
                                                                                  in0=psum[:, bk, 0:4 * ntok].rearrange("p (j n) -> p j n", n=ntok),
                                                                                  in1=rt[:, 0:4 * ntok].rearrange("p (j n) -> p j n", n=ntok), op=ALU.mult),
                      [pbank[bk], b_rt], [b_aT_of(4 * f4)])
                if f4 == 1:
                    while pending:
                        final_norm(*pending.pop(0))
        else:
            for fc in range(32):
                bk = bank()
                for k in range(8):
                    P.pe(lambda e, fc=fc, k=k, bk=bk, hT=hT, ntok=ntok: e.matmul(out=psum[:, bk, 0:ntok], lhsT=wf1[:, k, fc * 128:(fc + 1) * 128], rhs=hT[:, k, 0:ntok],
                                                                      start=(k == 0), stop=(k == 7)), [bw1[k][fc // 16], b_hT], [pbank[bk]])
                rt, b_rt = rts[fc % 2]
                P.act(lambda e, bk=bk, rt=rt, ntok=ntok: e.activation(out=rt[:, 0:ntok], in_=psum[:, bk, 0:ntok], func=AF.Relu), [pbank[bk]], [b_rt])
                P.dve(lambda e, fc=fc, bk=bk, rt=rt, ntok=ntok: e.tensor_tensor(out=aT[:, fc, 0:ntok], in0=psum[:, bk, 0:ntok], in1=rt[:, 0:ntok], op=ALU.mult),
                      [pbank[bk], b_rt], [b_aT_of(fc)])
                if fc == 7:
                    while pending:
                        final_norm(*pending.pop(0))
        if t + 1 < len(tiles):
            front(t + 1)
        subs = T["subs"]
        ffn2(t, *subs[0])
        if t + 1 < len(tiles):
            back(t + 1)
        xt0, b_xt0 = xt_of(t, 0)
        _norm_front(c, xt0[0:subs[0][1], :], b_xt0, subs[0][1], slots[0])
        if len(subs) > 1:
            ffn2(t, *subs[1], mid=(lambda hT=hT, b_hT=b_hT, r0_=subs[0][1]: _norm_back(c, r0_, 2, hT, b_hT, 0, slots[0])))
            xt1, b_xt1 = xt_of(t, 1)
            _norm_front(c, xt1[0:subs[1][1], :], b_xt1, subs[1][1], slots[1])
        gate_bk = {}
        wpl_bk = {}

        def ple_pT(s, rows):
            pt, b_pt = pts[s]; pT, b_pT = pTs[s]
            bk = bank()
            for k in range(2):
                P.pe(lambda e, k=k, rows=rows, bk=bk, pt=pt: e.transpose(out=psum[:, bk, k * 128:k * 128 + rows], in_=pt[0:rows, k * 128:(k + 1) * 128],
                                                                        identity=identf[0:rows, 0:rows]), [b_pt, b_identf], [pbank[bk]])
            P.act(lambda e, rows=rows, bk=bk, pT=pT: e.activation(out=pT[:, :, 0:rows], in_=psum[:, bk, 0:256].rearrange("p (k n) -> p k n", n=128)[:, :, 0:rows],
                                                                  func=AF.Copy), [pbank[bk]], [b_pT])

        def ple_gate(s, rows):
            bk = bank2()
            gate_bk[s] = bk
            for half in range(2):
                for k in range(8):
                    P.pe(lambda e, s=s, rows=rows, half=half, k=k, bk=bk, hT=hT: e.matmul(
                        out=psum[0:rows, bk + half, :], lhsT=hT[:, k, s * 128:s * 128 + rows], rhs=wpg[:, k, half * 512:(half + 1) * 512],
                        start=(k == 0), stop=(k == 7)), [b_hT, b_wpg], [pbank[bk + half]])
            sig, b_sig = sigs[s]
            P.act(lambda e, rows=rows, bk=bk, sig=sig: e.activation(out=sig[0:rows, :].rearrange("p (h n) -> p h n", n=512), in_=psum[0:rows, bk:bk + 2, :],
                                                                    func=AF.Sigmoid), [pbank[bk], pbank[bk + 1]], [b_sig])

        def ple_wpl(s, rows):
            pT, b_pT = pTs[s]
            bk = bank2()
            wpl_bk[s] = bk
            held.update((bk, bk + 1))
            for half in range(2):
                for k in range(2):
                    P.pe(lambda e, rows=rows, half=half, k=k, bk=bk, pT=pT: e.matmul(
                        out=psum[0:rows, bk + half, :], lhsT=pT[:, k, 0:rows], rhs=wpl[:, k, half * 512:(half + 1) * 512],
                        start=(k == 0), stop=(k == 1)), [b_pT, b_wpl], [pbank[bk + half]])

        if len(subs) == 2:
            (s0, r0_), (s1, r1_) = subs
            ple_pT(s0, r0_); ple_pT(s1, r1_)
            ple_gate(s0, r0_)
            ple_wpl(s0, r0_); ple_wpl(s1, r1_)
            _norm_back(c, r1_, 2, hT, b_hT, s1 * 128, slots[s1])
            ple_gate(s1, r1_)
        else:
            for (s, rows) in subs:
                _norm_back(c, rows, 2, hT, b_hT, s * 128, slots[s])
                ple_pT(s, rows)
                ple_gate(s, rows)
                ple_wpl(s, rows)
        for (s, rows) in subs:
            xt, b_xt = xt_of(t, s)
            sig, b_sig = sigs[s]
            bk = wpl_bk[s]
            held.difference_update((bk, bk + 1))
            P.dve(lambda e, rows=rows, bk=bk, sig=sig: e.tensor_tensor(out=sig[0:rows, :].rearrange("p (h n) -> p h n", n=512),
                                                                        in0=sig[0:rows, :].rearrange("p (h n) -> p h n", n=512),
                                                                        in1=psum[0:rows, bk:bk + 2, :], op=ALU.mult),
                  [pbank[bk], pbank[bk + 1], b_sig], [b_sig])
            P.dve(lambda e, rows=rows, xt=xt, sig=sig: e.tensor_tensor(out=xt[0:rows, :], in0=xt[0:rows, :], in1=sig[0:rows, :], op=ALU.add), [b_xt, b_sig], [b_xt])
        pending.append((t, subs))
    while pending:
        final_norm(*pending.pop(0))


_NC_CACHE = {}


def _consts():
    identf = np.eye(128, dtype=np.float32)
    pc = np.ones((128, 4, 16), dtype=np.float32)
    for g in range(4):
        w = 2 << g
        for t in range(16):
            pc[:, g, t] = float(w) / float(min(t + 1, w))
    return identf, pc


def kernel(x_prompt, x_sample, p_prompt, p_sample, state_pool, state_ssm_re, state_ssm_im,
           g_mix, w_in, w_pool, pool_scale, lam_re, lam_im, log_dt, b_re, b_im, c_re, c_im,
           d_skip, w_glu_v, w_glu_g, w_out, g_ff, w_ff1, w_ff2, g_ple, w_ple, w_ple_gate, g_final):
    f = lambda a: np.ascontiguousarray(np.asarray(a, dtype=np.float32))
    if "nc" not in _NC_CACHE:
        _NC_CACHE["nc"] = build_nc()
    nc = _NC_CACHE["nc"]
    identf, pc = _consts()
    shared = {
        "g_mix": f(g_mix)[0], "w_in": f(w_in)[0], "w_pool": f(w_pool)[0], "pool_scale": f(pool_scale)[0],
        "lam_re": f(lam_re)[0], "lam_im": f(lam_im)[0], "log_dt": f(log_dt)[0], "b_re": f(b_re)[0], "b_im": f(b_im)[0],
        "c_re": f(c_re)[0], "c_im": f(c_im)[0], "d_skip": f(d_skip)[0], "w_glu_v": f(w_glu_v)[0], "w_glu_g": f(w_glu_g)[0],
        "w_out": f(w_out)[0], "g_ff": f(g_ff)[0], "w_ff1": f(w_ff1)[0], "w_ff2": f(w_ff2)[0], "g_ple": f(g_ple)[0],
        "w_ple": f(w_ple)[0], "w_ple_gate": f(w_ple_gate)[0], "g_final": f(g_final), "identf": identf, "poolcorr": pc,
    }
    xp = f(x_prompt); xs = f(x_sample); pp = f(p_prompt); ps = f(p_sample)
    sp = f(state_pool); sr = f(state_ssm_re); si = f(state_ssm_im)
    in_maps = []
    for i in range(8):
        m = dict(shared)
        sl = slice(i * NSAMP, (i + 1) * NSAMP)
        m.update({"x": xp[i], "xs": np.ascontiguousarray(xs[sl, 0]), "p": pp[0, i], "ps": np.ascontiguousarray(ps[0, sl, 0]),
                  "spool": np.ascontiguousarray(sp[0, sl]), "sre": np.ascontiguousarray(sr[0, sl].reshape(NSAMP, 2048)),
                  "sim": np.ascontiguousarray(si[0, sl].reshape(NSAMP, 2048))})
        in_maps.append(m)
    res = run_bass_kernel_spmd(nc, in_maps, core_ids=list(range(8)))
    R = res.results
    y_prompt = np.stack([R[i]["y"] for i in range(8)], 0)
    y_sample = np.concatenate([R[i]["ys"] for i in range(8)], 0)[:, None, :]
    pool_p = np.stack([R[i]["pool_p"] for i in range(8)], 0)[None]
    pool_s = np.concatenate([R[i]["pool_s"] for i in range(8)], 0)[None]
    re_p = np.stack([R[i]["re_p"] for i in range(8)], 0)[None]
    im_p = np.stack([R[i]["im_p"] for i in range(8)], 0)[None]
    re_s = np.concatenate([R[i]["re_s"] for i in range(8)], 0).reshape(1, 128, 32, 64)
    im_s = np.concatenate([R[i]["im_s"] for i in range(8)], 0).reshape(1, 128, 32, 64)
    return (y_prompt, y_sample, pool_p, pool_s, re_p, im_p, re_s, im_s)
```
